# Optimizing a Trainium2 kernel written in Bass

```python
import math
import jax, jax.numpy as jnp
from jax import lax
import numpy as np

D_MODEL = 1024
BATCH = 4
SEQ = 8192
DEPTH = 1
DEC_BATCH = 32
DEC_SEQ = 2048
PAST_LEN = 128

GRID_W = 64
HEAD_DIM = 64
N_Q_HEADS = 8
N_KV_HEADS = 2
GQA_GROUP = N_Q_HEADS // N_KV_HEADS
Q_W = N_Q_HEADS * HEAD_DIM
KV_W = N_KV_HEADS * HEAD_DIM
AXIS_DIM = HEAD_DIM // 2
ROPE_THETA = 10000.0
Q_BLOCK = 128
CONV_W = D_MODEL // 2
CONV_K = 31
CONV_PAD = CONV_K // 2
MIX_W = Q_W + CONV_W
IN_W = Q_W + 2 * KV_W + 2 * CONV_W
D_FF = 4 * D_MODEL
EPS = 1e-6
LN_EPS = 1e-5

kernel_name = "hymba_gqa_axialrope_conformer_sqrelu_encoder"


def _rmsnorm(x, g):
    xf = x.astype(jnp.float32)
    y = xf * lax.rsqrt(jnp.mean(xf * xf, axis=-1, keepdims=True) + EPS)
    return (y * g.astype(jnp.float32)).astype(x.dtype)


def _layernorm(x, g, b):
    xf = x.astype(jnp.float32)
    mu = jnp.mean(xf, axis=-1, keepdims=True)
    xc = xf - mu
    var = jnp.mean(xc * xc, axis=-1, keepdims=True)
    y = xc * lax.rsqrt(var + LN_EPS) * g.astype(jnp.float32) + b.astype(jnp.float32)
    return y.astype(x.dtype)


def _axial_rope_tables(seq_len):
    rows = seq_len // GRID_W
    row = jnp.repeat(jnp.arange(rows, dtype=jnp.float32), GRID_W)
    col = jnp.tile(jnp.arange(GRID_W, dtype=jnp.float32), rows)
    inv_freq = ROPE_THETA ** (-jnp.arange(0, AXIS_DIM, 2, dtype=jnp.float32) / AXIS_DIM)
    ang_r = row[:, None] * inv_freq[None, :]
    ang_c = col[:, None] * inv_freq[None, :]
    return jnp.cos(ang_r), jnp.sin(ang_r), jnp.cos(ang_c), jnp.sin(ang_c)


def _rotate(x, cos, sin):
    half = x.shape[-1] // 2
    x1, x2 = x[..., :half], x[..., half:]
    c = cos[None, :, None, :]
    s = sin[None, :, None, :]
    return jnp.concatenate([x1 * c - x2 * s, x2 * c + x1 * s], axis=-1)


def _axial_rope(x, tables):
    cr, sr, cc, sc = tables
    xf = x.astype(jnp.float32)
    out = jnp.concatenate([_rotate(xf[..., :AXIS_DIM], cr, sr),
                           _rotate(xf[..., AXIS_DIM:], cc, sc)], axis=-1)
    return out.astype(x.dtype)


def _attention(q, k, v):
    B, S = q.shape[0], q.shape[1]
    nblk = S // Q_BLOCK
    qb = q.reshape(B, nblk, Q_BLOCK, N_KV_HEADS, GQA_GROUP, HEAD_DIM).transpose(1, 0, 2, 3, 4, 5)
    scale = 1.0 / math.sqrt(HEAD_DIM)

    def one_block(qblk):
        s = jnp.einsum('bqkgd,bskd->bkgqs', qblk, k,
                       preferred_element_type=jnp.float32) * scale
        p = jax.nn.softmax(s, axis=-1).astype(v.dtype)
        return jnp.einsum('bkgqs,bskd->bqkgd', p, v)

    o = lax.map(one_block, qb)
    return o.transpose(1, 0, 2, 3, 4, 5).reshape(B, S, Q_W)


def _conformer_conv(cv, cg, dw_w, dw_b, ln_g, ln_b):
    c = cv * jax.nn.sigmoid(cg)
    c = lax.conv_general_dilated(
        c, dw_w[:, None, :].astype(c.dtype), window_strides=(1,),
        padding=[(CONV_PAD, CONV_PAD)],
        dimension_numbers=('NWC', 'WIO', 'NWC'),
        feature_group_count=CONV_W) + dw_b
    return jax.nn.silu(_layernorm(c, ln_g, ln_b))


def _mixer(h, w_in, q_g, k_g, dw_w, dw_b, ln_g, ln_b, w_out, tables):
    B, S, _ = h.shape
    z = h @ w_in
    q, k, v, cv, cg = jnp.split(
        z, [Q_W, Q_W + KV_W, Q_W + 2 * KV_W, Q_W + 2 * KV_W + CONV_W], axis=-1)
    q = _axial_rope(_rmsnorm(q.reshape(B, S, N_Q_HEADS, HEAD_DIM), q_g), tables)
    k = _axial_rope(_rmsnorm(k.reshape(B, S, N_KV_HEADS, HEAD_DIM), k_g), tables)
    v = v.reshape(B, S, N_KV_HEADS, HEAD_DIM)
    a = _attention(q, k, v)
    c = _conformer_conv(cv, cg, dw_w, dw_b, ln_g, ln_b)
    return jnp.concatenate([a, c], axis=-1) @ w_out


def _mlp(h, w_up, w_down):
    u = h @ w_up
    return jnp.square(jax.nn.relu(u)) @ w_down


def _trunk(x, norm_mix_g, w_in, q_norm_g, k_norm_g, conv_dw_w, conv_dw_b,
           conv_ln_g, conv_ln_b, w_out, norm_mlp_g, w_up, w_down, norm_final_g):
    tables = _axial_rope_tables(x.shape[1])
    h = x
    for l in range(DEPTH):
        h = h + _mixer(_rmsnorm(h, norm_mix_g[l]), w_in[l], q_norm_g[l], k_norm_g[l],
                       conv_dw_w[l], conv_dw_b[l], conv_ln_g[l], conv_ln_b[l],
                       w_out[l], tables)
        h = h + _mlp(_rmsnorm(h, norm_mlp_g[l]), w_up[l], w_down[l])
    return _rmsnorm(h, norm_final_g)


def setup_inputs(seed: int = 0) -> dict:
    key = jax.random.key(seed)
    ks = jax.random.split(key, 16)
    f32 = jnp.float32

    def nrm(k, shape, scale):
        return jax.random.normal(k, shape, f32) * scale

    return {
        "x_prompt": nrm(ks[0], (BATCH, SEQ, D_MODEL), 1.0),
        "x_sample": nrm(ks[1], (DEC_BATCH, DEC_SEQ, D_MODEL), 1.0),
        "norm_mix_g": 1.0 + nrm(ks[2], (DEPTH, D_MODEL), 0.02),
        "w_in": nrm(ks[3], (DEPTH, D_MODEL, IN_W), D_MODEL ** -0.5),
        "q_norm_g": 1.0 + nrm(ks[4], (DEPTH, HEAD_DIM), 0.02),
        "k_norm_g": 1.0 + nrm(ks[5], (DEPTH, HEAD_DIM), 0.02),
        "conv_dw_w": nrm(ks[6], (DEPTH, CONV_K, CONV_W), CONV_K ** -0.5),
        "conv_dw_b": nrm(ks[7], (DEPTH, CONV_W), 0.02),
        "conv_ln_g": 1.0 + nrm(ks[8], (DEPTH, CONV_W), 0.02),
        "conv_ln_b": nrm(ks[9], (DEPTH, CONV_W), 0.02),
        "w_out": nrm(ks[10], (DEPTH, MIX_W, D_MODEL), MIX_W ** -0.5),
        "norm_mlp_g": 1.0 + nrm(ks[11], (DEPTH, D_MODEL), 0.02),
        "w_up": nrm(ks[12], (DEPTH, D_MODEL, D_FF), D_MODEL ** -0.5),
        "w_down": nrm(ks[13], (DEPTH, D_FF, D_MODEL), D_FF ** -0.5),
        "norm_final_g": 1.0 + nrm(ks[14], (D_MODEL,), 0.02),
    }


def reference(x_prompt, x_sample, norm_mix_g, w_in, q_norm_g, k_norm_g, conv_dw_w,
              conv_dw_b, conv_ln_g, conv_ln_b, w_out, norm_mlp_g, w_up, w_down,
              norm_final_g):
    y_prompt = _trunk(x_prompt, norm_mix_g, w_in, q_norm_g, k_norm_g, conv_dw_w,
                      conv_dw_b, conv_ln_g, conv_ln_b, w_out, norm_mlp_g, w_up,
                      w_down, norm_final_g)
    y_sample = _trunk(x_sample, norm_mix_g, w_in, q_norm_g, k_norm_g, conv_dw_w,
                      conv_dw_b, conv_ln_g, conv_ln_b, w_out, norm_mlp_g, w_up,
                      w_down, norm_final_g)
    return (y_prompt, y_sample)
```

```python
import contextlib
import numpy as np
import ml_dtypes
import concourse.bass as bass
import concourse.mybir as mybir
from concourse.bass_utils import run_bass_kernel_spmd

F32 = mybir.dt.float32
BF16 = mybir.dt.bfloat16
AF = mybir.ActivationFunctionType
ALU = mybir.AluOpType
AX = mybir.AxisListType

D = 1024
INW = 1792
QW = 512
CW = 512
HD = 64
CONV_K = 31
PAD = 15
EPS = 1e-6
LN_EPS = 1e-5
N_CORES = 8

ENGS = ("pe", "act", "dve", "pool", "sp")


def I(method, *args, **kw):
    return (method, args, kw)


class Buf:
    __slots__ = ("name", "w", "r")

    def __init__(self, name):
        self.name = name
        self.w = None
        self.r = []


class Sched:
    def __init__(self, nc, n_dma_sems=24):
        self.nc = nc
        self.ops = {e: [] for e in ENGS}
        self.sem_names = list(ENGS[:4]) + ["d%d" % i for i in range(n_dma_sems)]
        self.count = {s: 0 for s in self.sem_names}
        self.seen = {e: {s: 0 for s in self.sem_names} for e in ENGS}
        self.dma_rr = {}
        self.dma_pool = {"sp": (0, n_dma_sems * 2 // 3), "pool": (n_dma_sems * 2 // 3, n_dma_sems)}
        self.n_instr = {e: 0 for e in ENGS}
        self.n_wait = {e: 0 for e in ENGS}

    def _waits_for(self, eng, reads, writes):
        need = {}
        seen = self.seen[eng]

        def add(tok):
            s, v = tok
            if s == "pe" and eng == "pe":
                return
            if seen[s] < v and need.get(s, 0) < v:
                need[s] = v

        for b in reads:
            if b.w is not None:
                add(b.w)
        for b in writes:
            if b.w is not None:
                add(b.w)
            for t in b.r:
                add(t)
        for s, v in need.items():
            seen[s] = v
        return list(need.items())

    def _commit(self, tok, reads, writes):
        for b in reads:
            s = tok[0]
            b.r = [t for t in b.r if t[0] != s]
            b.r.append(tok)
        for b in writes:
            b.w = tok
            b.r = []

    def op(self, eng, fn, reads=(), writes=()):
        waits = self._waits_for(eng, reads, writes)
        self.count[eng] += 1
        tok = (eng, self.count[eng])
        self.ops[eng].append((waits, fn, (eng, 1)))
        self.n_instr[eng] += 1
        self.n_wait[eng] += len(waits)
        self._commit(tok, reads, writes)
        return tok

    def group(self, eng, fns, reads=(), writes=()):
        waits = self._waits_for(eng, reads, writes)
        self.count[eng] += 1
        tok = (eng, self.count[eng])
        n = len(fns)
        for i, fn in enumerate(fns):
            self.ops[eng].append((waits if i == 0 else [], fn, (eng, 1) if i == n - 1 else None))
        self.n_instr[eng] += n
        self.n_wait[eng] += len(waits)
        self._commit(tok, reads, writes)
        return tok

    def dma(self, q, fn, reads=(), writes=()):
        waits = self._waits_for(q, reads, writes)
        lo, hi = self.dma_pool[q]
        i = self.dma_rr.get(q, lo)
        sem = "d%d" % i
        self.dma_rr[q] = lo + ((i + 1 - lo) % (hi - lo))
        prev = self.count[sem]
        if prev > self.seen[q][sem]:
            self.seen[q][sem] = prev
            waits = [w for w in waits if w[0] != sem] + [(sem, prev)]
        self.count[sem] += 16
        tok = (sem, self.count[sem])
        self.ops[q].append((waits, fn, (sem, 16)))
        self.n_instr[q] += 1
        self.n_wait[q] += len(waits)
        self._commit(tok, reads, writes)
        return tok

    def final_wait(self, eng, bufs):
        waits = self._waits_for(eng, bufs, ())
        self.ops[eng].append((waits, None, None))

    def emit(self):
        nc = self.nc
        with contextlib.ExitStack() as st:
            sems = {s: st.enter_context(nc.semaphore("s_" + s)) for s in self.sem_names}
            block = st.enter_context(nc.Block())

            def run(e):
                def body(engh):
                    for waits, fn, inc in self.ops[e]:
                        for (s, v) in waits:
                            engh.wait_ge(sems[s], v)
                        if fn is None:
                            continue
                        ins = getattr(engh, fn[0])(*fn[1], **fn[2])
                        if inc is not None:
                            ins.then_inc(sems[inc[0]], inc[1])
                return body

            block.tensor(run("pe"))
            block.scalar(run("act"))
            block.vector(run("dve"))
            block.gpsimd(run("pool"))
            block.sync(run("sp"))


def default_cfg():
    return dict(NS=4, LS=2048, LP=8192, LQ=4096, QJ=1024, GT=512, DFF=4096)


def build_program(cfg):
    NS, LS, LP, LQ, QJ, GT, DFF = (cfg[k] for k in ("NS", "LS", "LP", "LQ", "QJ", "GT", "DFF"))
    NFF = DFF // 128
    NTJ = QJ // 128
    NGJ = QJ // GT
    TPG = GT // 128
    LMAX = max(LP, LS)
    NKC_MAX = LMAX // 128

    nc = bass.Bass("TRN2", target_bir_lowering=False)

    def din(name, shape, dt=F32):
        return nc.dram_tensor(name, list(shape), dt, kind="ExternalInput").ap()

    xs = din("xs", [NS * LS, D])
    xp = din("xp", [LP, D])
    xq = din("xq", [LQ + 256, D])
    tab = din("tab", [LMAX, 64])
    tabq = din("tabq", [LQ, 64])
    w_in_h = din("w_in_h", [128, 8, INW])
    w_out_h = din("w_out_h", [128, 8, D])
    w_up_h = din("w_up_h", [NFF, 128, 8, 128])
    w_down_h = din("w_down_h", [NFF, 128, D])
    gmix_h = din("gmix", [128, 8])
    gmlp_h = din("gmlp", [128, 8])
    gqk_h = din("gqk", [128, 640])
    gfin_h = din("gfin", [128, D])
    convw_h = din("convw", [128, 4, CONV_K])
    convb_h = din("convb", [128, 4])
    lng_h = din("lng", [128, 4])
    lnb_h = din("lnb", [128, 4])
    ident_h = din("ident", [128, 128], BF16)
    onesd_h = din("onesd", [128, 128], BF16)
    ys = nc.dram_tensor("ys", [NS * LS, D], F32, kind="ExternalOutput").ap()
    yp = nc.dram_tensor("yp", [LQ, D], F32, kind="ExternalOutput").ap()
    wus = nc.dram_tensor("wus", [NFF, 128, 1024], BF16, kind="Internal").ap()
    wds = nc.dram_tensor("wds", [NFF, 128, 1024], BF16, kind="Internal").ap()

    S = Sched(nc)
    st = contextlib.ExitStack()

    def sb(name, shape, dt):
        return st.enter_context(nc.sbuf_tensor(name, list(shape), dt))

    w_in_bf = sb("w_in_bf", [128, 8, INW], BF16)
    w_out_bf = sb("w_out_bf", [128, 8, D], BF16)
    ident = sb("ident_sb", [128, 128], BF16)
    onesd = sb("onesd_sb", [128, 128], BF16)
    gqk = sb("gqk_sb", [128, 640], F32)
    gfin = sb("gfin_sb", [128, D], F32)
    gmix = sb("gmix_sb", [128, 8], F32)
    gmlp = sb("gmlp_sb", [128, 8], F32)
    convw = sb("convw_sb", [128, 4, CONV_K], F32)
    convb = sb("convb_sb", [128, 4], F32)
    lng = sb("lng_sb", [128, 4], F32)
    lnb = sb("lnb_sb", [128, 4], F32)
    KT = sb("KT", [128, LMAX], BF16)
    V = sb("V", [128, NKC_MAX, 192], BF16)
    QT = sb("QT", [128, 4, QJ], BF16)
    CA = sb("CA", [128, 4, QJ], BF16)
    NPT = 3
    PT = [sb("PT%d" % i, [128, 1024], BF16) for i in range(NPT)]
    xt = [sb("xt%d" % i, [128, D], F32) for i in range(2)]
    tabt = [sb("tabt%d" % i, [128, 64], F32) for i in range(5)]
    xnb = [sb("xnb%d" % i, [128, D], BF16) for i in range(2)]
    xnT = sb("xnT", [128, 8, GT], BF16)
    ssq = [sb("ssq%d" % i, [128, 1], F32) for i in range(2)]
    rstd = [sb("rstd%d" % i, [128, 1], F32) for i in range(2)]
    ssq2 = [sb("ssq2_%d" % i, [128, 8], F32) for i in range(2)]
    rstd2 = [sb("rstd2_%d" % i, [128, 8], F32) for i in range(2)]
    rd = sb("rd", [128, 512], F32)
    dg = [sb("dg%d" % i, [128, 128], BF16) for i in range(4)]
    scr1 = sb("scr1", [128, 8], F32)
    rs2 = sb("rs2", [128, 8], F32)

    P1_BYTES = (640 * 4) * 4 + 320 * 4 * 2 + 640 * 2 * 2 + 4 * (QJ + 2 * PAD) * 4 + 4 * QJ * 4 + 4 * GT * 2 + GT * 4 * 3 + 2 * 2048 + 8 * GT * 2
    P3_BYTES = NFF * GT * 2 + 8 * 2048 + TPG * D * 4 + GT * 4 * 2
    PRO_BYTES = P1_BYTES + 2 * 2048 * 4 + 2 * 2048 * 2 if (4 * (QJ + 2 * PAD) * 4 + 4 * QJ * 4) < 24576 else 0
    R_BYTES = max(P1_BYTES, P3_BYTES, PRO_BYTES)
    R = sb("R", [128, R_BYTES // 4 + 8], F32)

    class Carver:
        def __init__(self):
            self.off = 0

        def take(self, nelem, dt):
            nbytes = nelem * (4 if dt == F32 else 2)
            nw = (nbytes + 3) // 4
            a = R[:, self.off:self.off + nw]
            self.off += nw
            assert self.off * 4 <= R_BYTES + 32, (self.off * 4, R_BYTES)
            return a if dt == F32 else a.bitcast(BF16)

    c1 = Carver()
    sq = [c1.take(640, F32) for _ in range(2)]
    zn = [c1.take(640, F32) for _ in range(2)]
    t1 = c1.take(320, F32)
    t2 = c1.take(320, F32)
    qkb = [c1.take(640, BF16) for _ in range(2)]
    cbuf_off = c1.off
    cbuf_flat = c1.take(4 * (QJ + 2 * PAD), F32)
    cbuf = cbuf_flat.rearrange("p (c t) -> p c t", c=4)
    cbufb = cbuf_flat.bitcast(BF16)[:, 0:4 * (QJ + 2 * PAD)].rearrange("p (c t) -> p c t", c=4)
    co = c1.take(4 * QJ, F32).rearrange("p (c t) -> p c t", c=4)
    cobf = c1.take(4 * GT, BF16).rearrange("p (c t) -> p c t", c=4)
    sig2 = c1.take(2 * GT, F32)
    sig = [sig2[:, 0:GT], sig2[:, GT:2 * GT]]
    assert QJ <= 2 * GT
    ctmp = sig2[:, 0:QJ]
    lnr = c1.take(GT, F32)
    xnTh = [c1.take(8 * 128, BF16).rearrange("p (k t) -> p k t", k=8) for _ in range(2)]
    xnT1 = c1.take(8 * GT, BF16).rearrange("p (k t) -> p k t", k=8)
    xnTg = [xnT, xnT1]

    c3 = Carver()
    u2T = c3.take(NFF * GT, BF16).rearrange("p (f t) -> p f t", f=NFF)
    wpool = [c3.take(1024, BF16) for _ in range(8)]
    hbuf = c3.take(TPG * D, F32).rearrange("p (t d) -> p t d", t=TPG)
    relu2 = c3.take(2 * GT, F32)
    relu = [relu2[:, 0:GT], relu2[:, GT:2 * GT]]

    c0 = Carver()
    c0.off = cbuf_off if (4 * (QJ + 2 * PAD) * 4 + 4 * QJ * 4) >= 24576 else c1.off
    stg = [c0.take(2048, F32) for _ in range(2)]
    stgb = [c0.take(2048, BF16) for _ in range(2)]

    ps = st.enter_context(nc.psum_tensor("ps", [128, 4096], F32))

    def bank(b, n=1):
        return ps[:, b * 512:(b + n) * 512]

    def bank_bf(b):
        return ps[:, b * 512:(b + 1) * 512].bitcast(BF16)

    PB = [Buf("pb%d" % i) for i in range(8)]
    RG = Buf("RG")
    B_win, B_wout, B_const = Buf("win"), Buf("wout"), Buf("const")
    B_KT = [Buf("KT%d" % i) for i in range(NKC_MAX)]
    B_V = [Buf("V%d" % i) for i in range(NKC_MAX)]
    B_QT = [[Buf("QT%d_%d" % (c, g)) for g in range(NGJ)] for c in range(4)]
    B_CA = [Buf("CA%d" % g) for g in range(NGJ)]
    B_PT = [Buf("PT%d" % i) for i in range(NPT)]
    B_xt = [Buf("xt%d" % i) for i in range(2)]
    B_tabt = [Buf("tabt%d" % i) for i in range(5)]
    B_xnb = [Buf("xnb%d" % i) for i in range(2)]
    B_xnT = Buf("xnT")
    B_xnTh = [Buf("xnTh%d" % i) for i in range(2)]
    B_xnTg = [B_xnT, Buf("xnT1")]
    B_ctmp = Buf("ctmp")
    B_st = [Buf("st%d" % i) for i in range(2)]
    B_st2 = [Buf("st2_%d" % i) for i in range(2)]
    B_rd = Buf("rd")
    B_dg = [Buf("dg%d" % i) for i in range(4)]
    B_sq, B_zn, B_qkb = [Buf("sq0"), Buf("sq1")], [Buf("zn0"), Buf("zn1")], [Buf("qkb0"), Buf("qkb1")]
    B_t1, B_t2 = Buf("t1"), Buf("t2")
    B_cbuf = [Buf("cbuf%d" % c) for c in range(4)]
    B_co = [Buf("co%d" % c) for c in range(4)]
    B_cobf, B_sig, B_lnr = Buf("cobf"), [Buf("sig0"), Buf("sig1")], Buf("lnr")
    B_u2T = [Buf("u2T%d" % f) for f in range(NFF)]
    B_wpool = [Buf("wp%d" % i) for i in range(8)]
    B_h = [Buf("h%d" % i) for i in range(TPG)]
    B_relu = [Buf("relu0"), Buf("relu1")]
    B_stg = [Buf("stg0"), Buf("stg1")]
    B_stgb = [Buf("stgb0"), Buf("stgb1")]
    B_wus = [Buf("wus%d" % f) for f in range(NFF)]
    B_wds = [Buf("wds%d" % f) for f in range(NFF)]
    B_out = [Buf("out%d" % i) for i in range(8)]
    octr = {"n": 0}
    B_scr = Buf("scr1")
    B_rs2 = [Buf("rs2_%d" % i) for i in range(8)]

    def region_switch():
        S.op("pool", I("memset", scr1[:], 0.0), reads=[], writes=[RG, B_scr])

    for dst, src in ((ident, ident_h), (onesd, onesd_h), (gqk, gqk_h), (gfin, gfin_h), (gmix, gmix_h),
                     (gmlp, gmlp_h), (convw, convw_h), (convb, convb_h), (lng, lng_h), (lnb, lnb_h)):
        S.dma("sp", I("dma_start", out=dst[:], in_=src), writes=[B_const])
    S.op("pool", I("memset", V[:, :, 64:128], 1.0), writes=B_V)

    sctr = {"n": 0}

    def nxt():
        j = sctr["n"] % 2
        sctr["n"] += 1
        return j

    for k in range(8):
        j = nxt()
        S.dma("sp", I("dma_start", out=stg[j][:, 0:INW], in_=w_in_h[:, k, :]), reads=[RG], writes=[B_stg[j]])
        S.op("dve", I("tensor_scalar", out=w_in_bf[:, k, :], in0=stg[j][:, 0:INW], scalar1=gmix[:, k:k + 1], scalar2=None,
                      op0=ALU.mult), reads=[RG, B_stg[j], B_const], writes=[B_win])
    for k in range(0, 8, 2):
        j = nxt()
        S.dma("sp", I("dma_start", out=stg[j][:, 0:2048].rearrange("p (k c) -> p k c", k=2), in_=w_out_h[:, k:k + 2, :]),
              reads=[RG], writes=[B_stg[j]])
        S.op("act", I("copy", out=w_out_bf[:, k:k + 2, :].rearrange("p k c -> p (k c)"), in_=stg[j][:, 0:2048]),
             reads=[RG, B_stg[j]], writes=[B_wout])

    FB = 2
    pro_thunks = []
    pend = {"out": None}

    def flush_out():
        if pend["out"] is not None:
            pend["out"]()
            pend["out"] = None

    for f0 in range(0, NFF, FB):
        def cv_up(f0=f0):
            j = nxt()
            S.dma("sp", I("dma_start", out=stg[j][:, 0:FB * 1024].rearrange("p (f n) -> p f n", f=FB),
                          in_=w_up_h[f0:f0 + FB].rearrange("f p k c -> p f (k c)")), reads=[RG], writes=[B_stg[j]])
            S.op("dve", I("tensor_tensor", out=stgb[j][:, 0:FB * 1024].rearrange("p (f k c) -> p f k c", f=FB, k=8),
                          in0=stg[j][:, 0:FB * 1024].rearrange("p (f k c) -> p f k c", f=FB, k=8),
                          in1=gmlp[:].unsqueeze(1).unsqueeze(3).to_broadcast([128, FB, 8, 128]), op=ALU.mult),
                 reads=[RG, B_stg[j], B_const], writes=[B_stgb[j]])
            flush_out()
            pend["out"] = lambda: S.dma("sp", I("dma_start", out=wus[f0:f0 + FB].rearrange("f p n -> p f n"),
                                                in_=stgb[j][:, 0:FB * 1024].rearrange("p (f n) -> p f n", f=FB)),
                                        reads=[RG, B_stgb[j]], writes=B_wus[f0:f0 + FB])
        pro_thunks.append(cv_up)
    for f0 in range(0, NFF, FB):
        def cv_dn(f0=f0):
            j = nxt()
            S.dma("sp", I("dma_start", out=stg[j][:, 0:FB * 1024].rearrange("p (f n) -> p f n", f=FB),
                          in_=w_down_h[f0:f0 + FB].rearrange("f p n -> p f n")), reads=[RG], writes=[B_stg[j]])
            S.op("act", I("copy", out=stgb[j][:, 0:FB * 1024], in_=stg[j][:, 0:FB * 1024]), reads=[RG, B_stg[j]], writes=[B_stgb[j]])
            flush_out()
            pend["out"] = lambda: S.dma("sp", I("dma_start", out=wds[f0:f0 + FB].rearrange("f p n -> p f n"),
                                                in_=stgb[j][:, 0:FB * 1024].rearrange("p (f n) -> p f n", f=FB)),
                                        reads=[RG, B_stgb[j]], writes=B_wds[f0:f0 + FB])
        pro_thunks.append(cv_dn)
    pro_thunks.append(flush_out)

    state = {"slot": 0}

    def pipeline(items, stages):
        ns = len(stages)
        for step in range(len(items) + ns - 1):
            for s_ in range(ns):
                i = step - s_
                if 0 <= i < len(items):
                    stages[s_](i, items[i])

    NTAB = len(tabt)

    def fe_a(i, x_ap, tab_ap):
        sl = i % 2
        S.dma("sp", I("dma_start", out=xt[sl][:], in_=x_ap), writes=[B_xt[sl]])
        if tab_ap is not None:
            S.dma("sp", I("dma_start", out=tabt[i % NTAB][:], in_=tab_ap), writes=[B_tabt[i % NTAB]])
        S.op("act", I("activation", out=xnb[sl][:], in_=xt[sl][:], func=AF.Square, accum_out=ssq[sl][:]),
             reads=[B_xt[sl]], writes=[B_xnb[sl], B_st[sl]])
        S.op("act", I("activation", out=rstd[sl][:], in_=ssq[sl][:], func=AF.Sqrt, bias=EPS, scale=1.0 / D),
             reads=[B_st[sl]], writes=[B_st[sl]])
        S.op("dve", I("reciprocal", out=rstd[sl][:], in_=rstd[sl][:]), reads=[B_st[sl]], writes=[B_st[sl]])
        S.op("dve", I("tensor_scalar", out=xnb[sl][:], in0=xt[sl][:], scalar1=rstd[sl][:, 0:1], scalar2=None, op0=ALU.mult),
             reads=[B_xt[sl], B_st[sl]], writes=[B_xnb[sl]])

    def fe_b(i, dstT, dstT_buf, cols):
        sl = i % 2
        tb = 2 + sl
        pv = bank_bf(tb)
        S.group("pe", [I("transpose", out=pv[:, k * 128:(k + 1) * 128], in_=xnb[sl][:, k * 128:(k + 1) * 128],
                         identity=ident[:]) for k in range(8)], reads=[B_xnb[sl], B_const], writes=[PB[tb]])
        S.op("dve", I("tensor_copy", out=dstT[:, :, cols], in_=pv.rearrange("p (k t) -> p k t", k=8)),
             reads=[RG, PB[tb]], writes=[dstT_buf])

    def qk_sq(i, zps, zbuf, nh):
        sl = i % 2
        S.op("act", I("activation", out=sq[sl][:, 0:nh * 64], in_=zps, func=AF.Square), reads=[RG, zbuf], writes=[B_sq[sl]])

    def qk_norm(i, zps, zbuf, nh, goff):
        sl = i % 2
        w = nh * 64
        S.op("dve", I("tensor_reduce", out=ssq2[sl][:, 0:nh], in_=sq[sl][:, 0:w].rearrange("p (h d) -> p h d", h=nh),
                      axis=AX.X, op=ALU.add), reads=[RG, B_sq[sl]], writes=[B_st2[sl]])
        S.op("act", I("activation", out=rstd2[sl][:, 0:nh], in_=ssq2[sl][:, 0:nh], func=AF.Sqrt, bias=EPS, scale=1.0 / HD),
             reads=[B_st2[sl]], writes=[B_st2[sl]])
        S.op("dve", I("reciprocal", out=rstd2[sl][:, 0:nh], in_=rstd2[sl][:, 0:nh]), reads=[B_st2[sl]], writes=[B_st2[sl]])
        S.op("dve", I("tensor_tensor", out=zn[sl][:, 0:w].rearrange("p (h d) -> p h d", h=nh),
                      in0=zps.rearrange("p (h d) -> p h d", h=nh),
                      in1=rstd2[sl][:, 0:nh].unsqueeze(2).to_broadcast([128, nh, HD]), op=ALU.mult),
             reads=[RG, zbuf, B_st2[sl]], writes=[B_zn[sl]])
        S.op("dve", I("tensor_tensor", out=zn[sl][:, 0:w], in0=zn[sl][:, 0:w], in1=gqk[:, goff:goff + w], op=ALU.mult),
             reads=[RG, B_zn[sl], B_const], writes=[B_zn[sl]])

    def rope(i, nh, re="pool"):
        sl = i % 2
        w = nh * 64
        tb_ = tabt[i % NTAB]
        btab = B_tabt[i % NTAB]
        zv = zn[sl][:, 0:w].rearrange("p (h a f i) -> p h a f i", h=nh, a=2, f=2)
        ov = qkb[sl][:, 0:w].rearrange("p (h a f i) -> p h a f i", h=nh, a=2, f=2)
        x1, x2 = zv[:, :, :, 0, :], zv[:, :, :, 1, :]
        o1, o2 = ov[:, :, :, 0, :], ov[:, :, :, 1, :]
        cosb = tb_[:, 0:32].rearrange("p (a i) -> p a i", a=2).unsqueeze(1).to_broadcast([128, nh, 2, 16])
        sinb = tb_[:, 32:64].rearrange("p (a i) -> p a i", a=2).unsqueeze(1).to_broadcast([128, nh, 2, 16])
        hw = nh * 32
        t1v = t1[:, 0:hw].rearrange("p (h a i) -> p h a i", h=nh, a=2)
        t2v = t2[:, 0:hw].rearrange("p (h a i) -> p h a i", h=nh, a=2)
        S.op(re, I("tensor_tensor", out=t1v, in0=x1, in1=cosb, op=ALU.mult), reads=[RG, B_zn[sl], btab], writes=[B_t1])
        S.op(re, I("tensor_tensor", out=t2v, in0=x2, in1=sinb, op=ALU.mult), reads=[RG, B_zn[sl], btab], writes=[B_t2])
        S.op(re, I("tensor_tensor", out=o1, in0=t1v, in1=t2v, op=ALU.subtract), reads=[RG, B_t1, B_t2], writes=[B_qkb[sl]])
        S.op(re, I("tensor_tensor", out=t1v, in0=x2, in1=cosb, op=ALU.mult), reads=[RG, B_zn[sl], btab], writes=[B_t1])
        S.op(re, I("tensor_tensor", out=t2v, in0=x1, in1=sinb, op=ALU.mult), reads=[RG, B_zn[sl], btab], writes=[B_t2])
        S.op(re, I("tensor_tensor", out=o2, in0=t1v, in1=t2v, op=ALU.add), reads=[RG, B_t1, B_t2], writes=[B_qkb[sl]])

    def ctx_stages(x_src, nt, tab_src, side=None):
        def stA1(i, it):
            t = it[1]
            fe_a(i, x_src(t), tab_src(t))
            if side:
                side.pop(0)()

        def stA2(i, it):
            fe_b(i, xnTh[i % 2], B_xnTh[i % 2], slice(0, 128))

        def stB1a(i, it):
            t = it[1]
            kb = i % 2
            S.group("pe", [I("matmul", bank(kb)[:, 0:256], lhsT=xnTh[i % 2][:, k, :], rhs=w_in_bf[:, k, 512:768],
                             start=(k == 0), stop=(k == 7)) for k in range(8)],
                    reads=[RG, B_xnTh[i % 2], B_win], writes=[PB[kb]])
            S.op("act", I("copy", out=V[:, t, :].rearrange("p (a d) -> p a d", a=3)[:, 0:3:2, :],
                          in_=bank(kb)[:, 128:256].rearrange("p (a d) -> p a d", a=2)), reads=[PB[kb]], writes=[B_V[t]])
            qk_sq(i, bank(kb)[:, 0:128], PB[kb], 2)

        def stB1b(i, it):
            kb = i % 2
            qk_norm(i, bank(kb)[:, 0:128], PB[kb], 2, 512)

        def stB2a(i, it):
            rope(i, 2)

        def stB2b(i, it):
            t = it[1]
            tb = 4 + i % 2
            pv = bank_bf(tb)
            S.op("pe", I("transpose", out=pv[:, 0:128], in_=qkb[i % 2][:, 0:128], identity=ident[:]),
                 reads=[RG, B_qkb[i % 2], B_const], writes=[PB[tb]])
            S.op("dve", I("tensor_copy", out=KT[:, t * 128:(t + 1) * 128], in_=pv[:, 0:128]), reads=[PB[tb]], writes=[B_KT[t]])

        return [("C", t) for t in range(nt)], [stA1, stA2, stB1a, stB1b, stB2a, stB2b]

    def glu_chunks(rhsT, rbuf, n, dst_col0, cdst):
        for c in range(4):
            bv, bg = (5 if c % 2 == 0 else 7), 6
            S.group("pe", [I("matmul", bank(bv)[:, 0:n], lhsT=w_in_bf[:, k, 768 + c * 128:768 + (c + 1) * 128],
                             rhs=rhsT[:, k, :], start=(k == 0), stop=(k == 7)) for k in range(8)],
                    reads=[RG, rbuf, B_win], writes=[PB[bv]])
            S.group("pe", [I("matmul", bank(bg)[:, 0:n], lhsT=w_in_bf[:, k, 1280 + c * 128:1280 + (c + 1) * 128],
                             rhs=rhsT[:, k, :], start=(k == 0), stop=(k == 7)) for k in range(8)],
                    reads=[RG, rbuf, B_win], writes=[PB[bg]])
            S.op("act", I("activation", out=sig[c % 2][:, 0:n], in_=bank(bg)[:, 0:n], func=AF.Sigmoid),
                 reads=[RG, PB[bg]], writes=[B_sig[c % 2]])
            S.op("dve", I("tensor_tensor", out=cdst[:, c, dst_col0:dst_col0 + n], in0=bank(bv)[:, 0:n],
                          in1=sig[c % 2][:, 0:n], op=ALU.mult), reads=[RG, PB[bv], B_sig[c % 2]], writes=[B_cbuf[c]])

    def job_stages(x_src, tab_src, has_left, has_right, cdst):
        items = []
        if has_left:
            items.append(("L", -1))
        else:
            S.op("pool", I("memset", cdst[:, :, 0:PAD], 0.0), reads=[RG], writes=B_cbuf)
        items += [("B", t) for t in range(NTJ)]
        if has_right:
            items.append(("R", NTJ))
        else:
            S.op("pool", I("memset", cdst[:, :, PAD + QJ:PAD + QJ + PAD], 0.0), reads=[RG], writes=B_cbuf)

        def stA1(i, it):
            kind, t = it
            fe_a(i, x_src(t), tab_src(t) if kind == "B" else None)

        def stA2(i, it):
            kind, t = it
            if kind == "B":
                g, tt = t // TPG, t % TPG
                fe_b(i, xnTg[g % 2], B_xnTg[g % 2], slice(tt * 128, (tt + 1) * 128))
            else:
                fe_b(i, xnTh[i % 2], B_xnTh[i % 2], slice(0, 128))

        def stB1a(i, it):
            kind, t = it
            if kind == "L":
                glu_chunks(xnTh[i % 2][:, :, 128 - PAD:128], B_xnTh[i % 2], PAD, 0, cdst)
            elif kind == "R":
                glu_chunks(xnTh[i % 2][:, :, 0:PAD], B_xnTh[i % 2], PAD, PAD + QJ, cdst)
            else:
                g, tt = t // TPG, t % TPG
                qb = i % 2
                S.group("pe", [I("matmul", bank(qb), lhsT=xnTg[g % 2][:, k, tt * 128:(tt + 1) * 128], rhs=w_in_bf[:, k, 0:512],
                                 start=(k == 0), stop=(k == 7)) for k in range(8)],
                        reads=[RG, B_xnTg[g % 2], B_win], writes=[PB[qb]])
                qk_sq(i, bank(qb), PB[qb], 8)
                if tt == TPG - 1:
                    glu_chunks(xnTg[g % 2][:, :, :], B_xnTg[g % 2], GT, PAD + g * GT, cdst)

        def stB1b(i, it):
            kind, t = it
            if kind == "B":
                qk_norm(i, bank(i % 2), PB[i % 2], 8, 0)

        def stB2a(i, it):
            if it[0] == "B":
                rope(i, 8)

        def stB2b(i, it):
            kind, t = it
            if kind != "B":
                return
            g = t // TPG
            pv = bank_bf(4)
            S.group("pe", [I("transpose", out=pv[:, c * 128:(c + 1) * 128], in_=qkb[i % 2][:, c * 128:(c + 1) * 128],
                             identity=ident[:]) for c in range(4)], reads=[RG, B_qkb[i % 2], B_const], writes=[PB[4]])
            S.op("act", I("copy", out=QT[:, :, t * 128:(t + 1) * 128], in_=pv[:, 0:512].rearrange("p (c t) -> p c t", c=4)),
                 reads=[PB[4]], writes=[B_QT[c][g] for c in range(4)])

        return items, [stA1, stA2, stB1a, stB1b, stB2a, stB2b]

    def conv_ln_thunks(do_conv=True):
        th = []
        POOL_CHUNKS = cfg.get("pool_chunks", ())
        per_chunk = []
        for c in (range(4) if do_conv else []):
            lst = []
            if c in POOL_CHUNKS:
                lst.append(lambda fb, c=c: S.op("pool", I("tensor_scalar", out=co[:, c, :], in0=cbuf[:, c, 0:QJ],
                                                          scalar1=convw[:, c, 0:1], scalar2=convb[:, c:c + 1], op0=ALU.mult, op1=ALU.add),
                                                reads=[RG, B_cbuf[c], B_const], writes=[B_co[c]]))
                for j in range(1, CONV_K):
                    def tap(fb, c=c, j=j):
                        S.op("pool", I("tensor_scalar", out=ctmp[:], in0=cbuf[:, c, j:j + QJ], scalar1=convw[:, c, j:j + 1],
                                       scalar2=None, op0=ALU.mult), reads=[RG, B_cbuf[c], B_const], writes=B_sig)
                        S.op("pool", I("tensor_tensor", out=co[:, c, :], in0=co[:, c, :], in1=ctmp[:], op=ALU.add),
                             reads=[RG, B_co[c]] + B_sig, writes=[B_co[c]])
                    lst.append(tap)
            else:
                lst.append(lambda fb, c=c: S.op("dve", I("tensor_scalar", out=co[:, c, :], in0=cbuf[:, c, 0:QJ],
                                                         scalar1=convw[:, c, 0:1], scalar2=convb[:, c:c + 1], op0=ALU.mult, op1=ALU.add),
                                                reads=[RG, B_cbuf[c], B_const], writes=[B_co[c]]))
                for j in range(1, CONV_K):
                    lst.append(lambda fb, c=c, j=j: S.op("dve", I("scalar_tensor_tensor", out=co[:, c, :], in0=cbuf[:, c, j:j + QJ],
                                                                 scalar=convw[:, c, j:j + 1], in1=co[:, c, :], op0=ALU.mult, op1=ALU.add),
                                                         reads=[RG, B_cbuf[c], B_const, B_co[c]], writes=[B_co[c]]))
            per_chunk.append(lst)
        dve_l = [x for c in range(len(per_chunk)) if c not in POOL_CHUNKS for x in per_chunk[c]]
        pool_l = [x for c in range(len(per_chunk)) if c in POOL_CHUNKS for x in per_chunk[c]]
        n_d, n_p = len(dve_l), len(pool_l)
        pi = 0
        for di, x in enumerate(dve_l):
            th.append(x)
            while pi < n_p and (pi + 1) * n_d <= (di + 1) * n_p:
                th.append(pool_l[pi])
                pi += 1
        th += pool_l[pi:]
        for g in range(NGJ if cfg.get("noln") is None else 0):
            gs = slice(g * GT, (g + 1) * GT)
            th.append(lambda fb, gs=gs: S.op("dve", I("tensor_copy", out=cobf[:], in_=co[:, :, gs]),
                                         reads=[RG] + B_co, writes=[B_cobf]))
            def mean_sub(fb, gs=gs):
                S.group("pe", [I("matmul", bank(fb)[:, 0:GT], lhsT=onesd[:], rhs=cobf[:, c, :],
                               start=(c == 0), stop=(c == 3)) for c in range(4)],
                        reads=[RG, B_cobf, B_const], writes=[PB[fb]])
                S.op("dve", I("tensor_tensor", out=co[:, :, gs], in0=co[:, :, gs],
                              in1=bank(fb)[:, 0:GT].unsqueeze(1).to_broadcast([128, 4, GT]), op=ALU.subtract),
                     reads=[RG, PB[fb]] + B_co, writes=B_co)
            th.append(mean_sub)
            th.append(lambda fb, gs=gs: S.op("act", I("activation", out=cobf[:], in_=co[:, :, gs], func=AF.Square),
                                         reads=[RG] + B_co, writes=[B_cobf]))
            def var_sqrt(fb):
                S.group("pe", [I("matmul", bank(fb)[:, 0:GT], lhsT=onesd[:], rhs=cobf[:, c, :],
                               start=(c == 0), stop=(c == 3)) for c in range(4)],
                        reads=[RG, B_cobf, B_const], writes=[PB[fb]])
                S.op("act", I("activation", out=lnr[:], in_=bank(fb)[:, 0:GT], func=AF.Sqrt, bias=LN_EPS, scale=1.0),
                     reads=[RG, PB[fb]], writes=[B_lnr])
            th.append(var_sqrt)
            th.append(lambda fb: S.op("dve", I("reciprocal", out=lnr[:], in_=lnr[:]), reads=[RG, B_lnr], writes=[B_lnr]))
            th.append(lambda fb, gs=gs: S.op("dve", I("tensor_tensor",
                out=co[:, :, gs], in0=co[:, :, gs], in1=lnr[:].unsqueeze(1).to_broadcast([128, 4, GT]), op=ALU.mult),
                reads=[RG, B_lnr] + B_co, writes=B_co))
            for c in range(4):
                th.append(lambda fb, c=c, gs=gs, g=g: S.op("act", I("activation",
                    out=CA[:, c, gs], in_=co[:, c, gs], func=AF.Silu, scale=lng[:, c:c + 1], bias=lnb[:, c:c + 1]),
                    reads=[RG, B_co[c], B_const], writes=[B_CA[g]]))
        if cfg.get('lnsteps') is not None:
            th = th[:4 * CONV_K + cfg['lnsteps']]
        return th

    def conv_pe():
        dctr = 0
        for c in range(4):
            b0 = 2 * (c % 2)
            for j in range(CONV_K):
                sl = dctr % 4
                dctr += 1
                S.op("act", I("activation", out=dg[sl][:], in_=ident[:], func=AF.Identity, scale=convw[:, c, j:j + 1]),
                     reads=[B_const], writes=[B_dg[sl]])
                S.group("pe", [I("matmul", bank(b0 + g)[:, 0:GT], lhsT=dg[sl][:], rhs=cbufb[:, c, g * GT + j:g * GT + j + GT],
                                 start=(j == 0), stop=(j == CONV_K - 1)) for g in range(NGJ)],
                        reads=[RG, B_dg[sl], B_cbuf[c]], writes=[PB[b0 + g] for g in range(NGJ)])
            for g in range(NGJ):
                S.op("dve", I("tensor_scalar", out=co[:, c, g * GT:(g + 1) * GT], in0=bank(b0 + g)[:, 0:GT],
                              scalar1=convb[:, c:c + 1], scalar2=None, op0=ALU.add),
                     reads=[RG, PB[b0 + g], B_const], writes=[B_co[c]])

    def attn_phase(nkc, side):
        its = [(hp, g, kc) for hp in range(4) for g in range(NGJ) for kc in range(nkc)]
        n_it = len(its)
        n_side = len(side)
        emitted = 0

        def emit_S(i):
            hp, g, kc = its[i]
            qs = slice(g * GT, (g + 1) * GT)
            ks = slice(kc * 128, (kc + 1) * 128)
            sb_ = 2 * (i % 2)
            S.group("pe", [
                I("matmul", bank(sb_)[:, 0:GT], lhsT=KT[0:64, ks], rhs=QT[0:64, hp, qs], start=True, stop=True,
                  tile_position=(0, 0)),
                I("matmul", bank(sb_ + 1)[:, 0:GT], lhsT=KT[64:128, ks], rhs=QT[64:128, hp, qs], start=True, stop=True,
                  tile_position=(64, 0)),
            ], reads=[B_KT[kc], B_QT[hp][g]], writes=[PB[sb_], PB[sb_ + 1]])

        for i0 in range(min(2, n_it)):
            emit_S(i0)
        for i in range(n_it):
            hp, g, kc = its[i]
            qs = slice(g * GT, (g + 1) * GT)
            ob = 4 + 2 * ((hp * NGJ + g) % 2)
            oA, oB = bank(ob), bank(ob + 1)
            sb_ = 2 * (i % 2)
            pt = PT[i % NPT]
            bpt = B_PT[i % NPT]
            if GT == 512:
                S.op("act", I("activation", out=pt[:], in_=bank(sb_, 2), func=AF.Exp, scale=0.125),
                     reads=[PB[sb_], PB[sb_ + 1]], writes=[bpt])
            else:
                S.op("act", I("activation", out=pt[:].rearrange("p (a t) -> p a t", a=2)[:, :, 0:GT],
                              in_=bank(sb_, 2).rearrange("p (a t) -> p a t", a=2)[:, :, 0:GT], func=AF.Exp, scale=0.125),
                     reads=[PB[sb_], PB[sb_ + 1]], writes=[bpt])
            want = ((i + 1) * n_side + n_it - 1) // n_it
            while emitted < want and side:
                side.pop(0)(sb_)
                emitted += 1
            if i + 2 < n_it:
                emit_S(i + 2)
            S.group("pe", [
                I("matmul", oA[:, 0:GT], lhsT=V[:, kc, 0:128], rhs=pt[:, 0:GT], start=(kc == 0), stop=(kc == nkc - 1)),
                I("matmul", oB[:, 0:GT], lhsT=V[:, kc, 64:192], rhs=pt[:, 512:512 + GT], start=(kc == 0), stop=(kc == nkc - 1)),
            ], reads=[B_V[kc], bpt], writes=[PB[ob], PB[ob + 1]])
            if kc == nkc - 1:
                S.op("dve", I("reciprocal", out=rd[64:128, 0:GT], in_=oA[64:128, 0:GT]), reads=[PB[ob]], writes=[B_rd])
                S.op("dve", I("reciprocal", out=rd[0:64, 0:GT], in_=oB[0:64, 0:GT]), reads=[PB[ob + 1]], writes=[B_rd])
                S.op("dve", I("tensor_tensor", out=QT[0:64, hp, qs], in0=oA[0:64, 0:GT], in1=rd[64:128, 0:GT], op=ALU.mult),
                     reads=[PB[ob], B_rd], writes=[B_QT[hp][g]])
                S.op("dve", I("tensor_tensor", out=QT[64:128, hp, qs], in0=oB[64:128, 0:GT], in1=rd[0:64, 0:GT], op=ALU.mult),
                     reads=[PB[ob + 1], B_rd], writes=[B_QT[hp][g]])
        while side:
            side.pop(0)(0)

    wctr = {"n": 0}
    NWS = len(wpool)

    def stream_w(src_ap, src_buf):
        i = wctr["n"] % NWS
        wctr["n"] += 1
        S.dma("sp", I("dma_start", out=wpool[i][:], in_=src_ap), reads=[RG, src_buf], writes=[B_wpool[i]])
        return wpool[i], B_wpool[i]

    def mlp_phase(x_src, y_dst):
        for g in range(NGJ):
            def outproj(tt):
                t = g * TPG + tt
                ts_ = slice(t * 128, (t + 1) * 128)
                ob = 2 * tt
                fns = []
                for half in range(2):
                    for c in range(8):
                        lhs = QT[:, c, ts_] if c < 4 else CA[:, c - 4, ts_]
                        fns.append(I("matmul", bank(ob + half), lhsT=lhs, rhs=w_out_bf[:, c, half * 512:(half + 1) * 512],
                                     start=(c == 0), stop=(c == 7)))
                S.group("pe", fns, reads=[B_QT[c][g] for c in range(4)] + [B_CA[g], B_wout], writes=[PB[ob], PB[ob + 1]])
                sl = state["slot"]
                state["slot"] ^= 1
                S.dma("sp", I("dma_start", out=xt[sl][:], in_=x_src(t)), writes=[B_xt[sl]])
                S.op("dve", I("tensor_tensor", out=hbuf[:, tt, :], in0=bank(ob, 2), in1=xt[sl][:], op=ALU.add),
                     reads=[RG, PB[ob], PB[ob + 1], B_xt[sl]], writes=[B_h[tt]])

            def cast_T(tt):
                sl = tt % 2
                S.op("dve", I("tensor_copy", out=xnb[sl][:], in_=hbuf[:, tt, :]), reads=[RG, B_h[tt]], writes=[B_xnb[sl]])
                tb = 2 * tt
                pv = bank_bf(tb)
                S.group("pe", [I("transpose", out=pv[:, k * 128:(k + 1) * 128], in_=xnb[sl][:, k * 128:(k + 1) * 128],
                                 identity=ident[:]) for k in range(8)], reads=[B_xnb[sl], B_const], writes=[PB[tb]])
                S.op("act", I("copy", out=xnT[:, :, tt * 128:(tt + 1) * 128], in_=pv.rearrange("p (k t) -> p k t", k=8)),
                     reads=[PB[tb]], writes=[B_xnT])

            def stats(tt):
                junk = relu2.bitcast(BF16)
                S.op("act", I("activation", out=junk[:, 0:D], in_=hbuf[:, tt, :], func=AF.Square, accum_out=rs2[:, tt:tt + 1]),
                     reads=[RG, B_h[tt]], writes=[B_relu[0], B_relu[1], B_rs2[tt]])
                S.op("dve", I("tensor_scalar", out=rs2[:, tt:tt + 1], in0=rs2[:, tt:tt + 1], scalar1=1.0 / D, scalar2=EPS,
                              op0=ALU.mult, op1=ALU.add), reads=[B_rs2[tt]], writes=[B_rs2[tt]])
                S.op("dve", I("reciprocal", out=rs2[:, tt:tt + 1], in_=rs2[:, tt:tt + 1]), reads=[B_rs2[tt]], writes=[B_rs2[tt]])

            outproj(0)
            for tt in range(TPG):
                if tt + 1 < TPG:
                    outproj(tt + 1)
                cast_T(tt)
            for tt in range(TPG):
                stats(tt)
            for f in range(NFF):
                w_ap, w_buf = stream_w(wus[f], B_wus[f])
                wv = w_ap[:].rearrange("p (k c) -> p k c", k=8)
                ub = 4 + f % 4
                S.group("pe", [I("matmul", bank(ub)[:, 0:GT], lhsT=wv[:, k, :], rhs=xnT[:, k, :], start=(k == 0), stop=(k == 7))
                               for k in range(8)], reads=[RG, w_buf, B_xnT], writes=[PB[ub]])
                if f % 2 == 0:
                    S.op("act", I("activation", out=relu[0][:], in_=bank(ub)[:, 0:GT], func=AF.Relu), reads=[RG, PB[ub]], writes=[B_relu[0]])
                    S.op("act", I("activation", out=u2T[:, f, :], in_=relu[0][:], func=AF.Square), reads=[RG, B_relu[0]], writes=[B_u2T[f]])
                else:
                    S.op("dve", I("tensor_scalar", out=relu[1][:], in0=bank(ub)[:, 0:GT], scalar1=0.0, scalar2=None, op0=ALU.max),
                         reads=[RG, PB[ub]], writes=[B_relu[1]])
                    S.op("dve", I("tensor_tensor", out=u2T[:, f, :], in0=relu[1][:], in1=relu[1][:], op=ALU.mult),
                         reads=[RG, B_relu[1]], writes=[B_u2T[f]])
            for f in range(NFF):
                w_ap, w_buf = stream_w(wds[f], B_wds[f])
                fns = []
                for tt in range(TPG):
                    for half in range(2):
                        fns.append(I("matmul", bank(2 * tt + half), lhsT=u2T[:, f, tt * 128:(tt + 1) * 128],
                                     rhs=w_ap[:, half * 512:(half + 1) * 512], start=(f == 0), stop=(f == NFF - 1)))
                S.group("pe", fns, reads=[RG, B_u2T[f], w_buf], writes=[PB[i] for i in range(2 * TPG)])
            for tt in range(TPG):
                t = g * TPG + tt
                sl = tt % 2
                S.op("dve", I("scalar_tensor_tensor", out=hbuf[:, tt, :], in0=bank(2 * tt, 2), scalar=rs2[:, tt:tt + 1],
                              in1=hbuf[:, tt, :], op0=ALU.mult, op1=ALU.add),
                     reads=[RG, PB[2 * tt], PB[2 * tt + 1], B_h[tt], B_rs2[tt]], writes=[B_h[tt]])
                S.op("act", I("activation", out=xnb[sl][:], in_=hbuf[:, tt, :], func=AF.Square, accum_out=ssq[sl][:]),
                     reads=[RG, B_h[tt]], writes=[B_xnb[sl], B_st[sl]])
                S.op("act", I("activation", out=rstd[sl][:], in_=ssq[sl][:], func=AF.Sqrt, bias=EPS, scale=1.0 / D),
                     reads=[B_st[sl]], writes=[B_st[sl]])
                S.op("dve", I("reciprocal", out=rstd[sl][:], in_=rstd[sl][:]), reads=[B_st[sl]], writes=[B_st[sl]])
                S.op("dve", I("scalar_tensor_tensor", out=hbuf[:, tt, :], in0=hbuf[:, tt, :], scalar=rstd[sl][:, 0:1],
                              in1=gfin[:], op0=ALU.mult, op1=ALU.mult),
                     reads=[RG, B_h[tt], B_st[sl], B_const], writes=[B_h[tt]])
                S.dma("pool", I("dma_start", out=y_dst(t), in_=hbuf[:, tt, :]),
                      reads=[RG, B_h[tt]], writes=[B_out[octr["n"] % 8]])
                octr["n"] += 1

    def run_context(ctx_x, ctx_nt, jobs, pe_conv_ctx=False):
        if cfg.get("stop") == "pro":
            return
        first = bool(pro_thunks)
        c_items, c_st = ctx_stages(ctx_x, ctx_nt, lambda t: tab[t * 128:(t + 1) * 128, :], side=pro_thunks)
        if first or not jobs or cfg.get("stop") in ("ctx",):
            pipeline(c_items, c_st)
            c_items = []
            while pro_thunks:
                pro_thunks.pop(0)()
            region_switch()
        if cfg.get("stop") == "ctx":
            return
        for ji, (x_src, tab_src, y_dst, hl, hr) in enumerate(jobs):
            pe_conv_job = pe_conv_ctx and NGJ <= 2
            j_items, j_st = job_stages(x_src, tab_src, hl, hr, cbufb if pe_conv_job else cbuf)
            if ji == 0 and c_items:
                pipeline(c_items + j_items,
                         [(lambda i, it, a=a, b=b: (a if it[0] == "C" else b)(i, it)) for a, b in zip(c_st, j_st)])
            else:
                pipeline(j_items, j_st)
            if cfg.get("stop") == "p1":
                continue
            if pe_conv_job:
                conv_pe()
                for th_ in conv_ln_thunks(do_conv=False):
                    th_(4)
                side = []
            else:
                side = conv_ln_thunks()
            if cfg.get("stop") == "conv":
                while side:
                    side.pop(0)(0)
                continue
            attn_phase(ctx_nt, side)
            if cfg.get("stop") == "attn":
                continue
            region_switch()
            mlp_phase(x_src, y_dst)
            region_switch()

    if LP > 0:
        jobs = []
        for j in range(LQ // QJ):
            base = 128 + j * QJ
            jobs.append((
                (lambda t, base=base: xq[base + t * 128: base + (t + 1) * 128, :]),
                (lambda t, j=j: tabq[j * QJ + t * 128: j * QJ + (t + 1) * 128, :]),
                (lambda t, j=j: yp[j * QJ + t * 128: j * QJ + (t + 1) * 128, :]),
                True, True))
        run_context(lambda t: xp[t * 128:(t + 1) * 128, :], LP // 128, jobs)
    for s in range(NS):
        jobs = []
        nj = LS // QJ
        for j in range(nj):
            base = s * LS + j * QJ
            jobs.append((
                (lambda t, base=base: xs[base + t * 128: base + (t + 1) * 128, :]),
                (lambda t, j=j: tab[j * QJ + t * 128: j * QJ + (t + 1) * 128, :]),
                (lambda t, base=base: ys[base + t * 128: base + (t + 1) * 128, :]),
                j > 0, j < nj - 1))
        run_context(lambda t, s=s: xs[s * LS + t * 128: s * LS + (t + 1) * 128, :], LS // 128, jobs,
                    pe_conv_ctx=cfg.get("pe_conv", True))

    S.final_wait("pool", B_out)
    S.final_wait("sp", B_out)
    S.emit()
    st.close()
    return nc, S


def rope_table(npos):
    t = np.arange(npos)
    row = (t // 64).astype(np.float32)
    col = (t % 64).astype(np.float32)
    inv_freq = (np.float32(10000.0) ** (-(np.arange(0, 32, 2, dtype=np.float32)) / np.float32(32))).astype(np.float32)
    ar = row[:, None] * inv_freq[None, :]
    ac = col[:, None] * inv_freq[None, :]
    return np.concatenate([np.cos(ar), np.cos(ac), np.sin(ar), np.sin(ac)], axis=1).astype(np.float32)


def prep_shared(cfg, norm_mix_g, w_in, q_norm_g, k_norm_g, conv_dw_w, conv_dw_b, conv_ln_g, conv_ln_b,
                w_out, norm_mlp_g, w_up, w_down, norm_final_g):
    DFF = cfg["DFF"]
    NFF = DFF // 128
    qperm = np.concatenate([np.concatenate([np.arange(c * 64, (c + 1) * 64), np.arange((4 + c) * 64, (5 + c) * 64)])
                            for c in range(4)])
    colperm = np.concatenate([qperm, np.arange(512, INW)])
    w_in_p = np.asarray(w_in[0])[:, colperm]
    rowperm = np.concatenate([qperm, np.arange(512, 1024)])
    w_out_p = np.asarray(w_out[0])[rowperm, :]
    sh = {}
    sh["w_in_h"] = np.ascontiguousarray(w_in_p.reshape(8, 128, INW).transpose(1, 0, 2))
    sh["w_out_h"] = np.ascontiguousarray(w_out_p.reshape(8, 128, D).transpose(1, 0, 2))
    sh["w_up_h"] = np.ascontiguousarray(np.asarray(w_up[0]).reshape(8, 128, NFF, 128).transpose(2, 1, 0, 3))
    sh["w_down_h"] = np.ascontiguousarray(np.asarray(w_down[0]).reshape(NFF, 128, D))
    sh["gmix"] = np.ascontiguousarray(np.asarray(norm_mix_g[0]).reshape(8, 128).T)
    sh["gmlp"] = np.ascontiguousarray(np.asarray(norm_mlp_g[0]).reshape(8, 128).T)
    gq = np.tile(np.asarray(q_norm_g[0]), 8)
    gk = np.tile(np.asarray(k_norm_g[0]), 2)
    sh["gqk"] = np.ascontiguousarray(np.broadcast_to(np.concatenate([gq, gk])[None, :], (128, 640))).astype(np.float32)
    sh["gfin"] = np.ascontiguousarray(np.broadcast_to(np.asarray(norm_final_g)[None, :], (128, D))).astype(np.float32)
    sh["convw"] = np.ascontiguousarray(np.asarray(conv_dw_w[0]).reshape(CONV_K, 4, 128).transpose(2, 1, 0))
    sh["convb"] = np.ascontiguousarray(np.asarray(conv_dw_b[0]).reshape(4, 128).T)
    sh["lng"] = np.ascontiguousarray(np.asarray(conv_ln_g[0]).reshape(4, 128).T)
    sh["lnb"] = np.ascontiguousarray(np.asarray(conv_ln_b[0]).reshape(4, 128).T)
    sh["ident"] = np.eye(128, dtype=np.float32).astype(ml_dtypes.bfloat16)
    sh["onesd"] = np.full((128, 128), 1.0 / CW, dtype=np.float32).astype(ml_dtypes.bfloat16)
    return {k: (v if v.dtype == ml_dtypes.bfloat16 else np.ascontiguousarray(v, dtype=np.float32)) for k, v in sh.items()}


_PROGRAM_CACHE = {}


def kernel(x_prompt, x_sample, norm_mix_g, w_in, q_norm_g, k_norm_g, conv_dw_w, conv_dw_b, conv_ln_g, conv_ln_b,
           w_out, norm_mlp_g, w_up, w_down, norm_final_g):
    cfg = default_cfg()
    NS, LS, LP, LQ = cfg["NS"], cfg["LS"], cfg["LP"], cfg["LQ"]
    x_prompt = np.asarray(x_prompt, dtype=np.float32)
    x_sample = np.asarray(x_sample, dtype=np.float32)
    sh = prep_shared(cfg, norm_mix_g, w_in, q_norm_g, k_norm_g, conv_dw_w, conv_dw_b, conv_ln_g, conv_ln_b,
                     w_out, norm_mlp_g, w_up, w_down, norm_final_g)
    tab_full = rope_table(max(LP, LS))
    in_maps = []
    zeros_tile = np.zeros((128, D), np.float32)
    for c in range(N_CORES):
        b, half = c // 2, c % 2
        seq = x_prompt[b]
        own = seq[half * LQ:(half + 1) * LQ]
        left = seq[half * LQ - 128: half * LQ] if half == 1 else zeros_tile
        right = seq[(half + 1) * LQ:(half + 1) * LQ + 128] if half == 0 else zeros_tile
        m = dict(sh)
        m["xs"] = np.ascontiguousarray(x_sample[c * NS:(c + 1) * NS].reshape(NS * LS, D))
        m["xp"] = np.ascontiguousarray(seq)
        m["xq"] = np.ascontiguousarray(np.concatenate([left, own, right], axis=0))
        m["tab"] = tab_full
        m["tabq"] = np.ascontiguousarray(tab_full[half * LQ:(half + 1) * LQ])
        in_maps.append(m)
    if "nc" not in _PROGRAM_CACHE:
        _PROGRAM_CACHE["nc"] = build_program(cfg)[0]
    nc = _PROGRAM_CACHE["nc"]
    res = run_bass_kernel_spmd(nc, in_maps, core_ids=list(range(N_CORES)))
    y_prompt = np.empty((4, 2 * LQ, D), np.float32)
    y_sample = np.empty((N_CORES * NS, LS, D), np.float32)
    for c in range(N_CORES):
        r = res.results[c]
        b, half = c // 2, c % 2
        y_prompt[b, half * LQ:(half + 1) * LQ] = np.asarray(r["yp"]).reshape(LQ, D)
        y_sample[c * NS:(c + 1) * NS] = np.asarray(r["ys"]).reshape(NS, LS, D)
    return (y_prompt, y_sample)
```

```python
import contextlib
import numpy as np
import ml_dtypes
import concourse.bass as bass
import concourse.mybir as mybir
from concourse.bass_utils import run_bass_kernel_spmd

F32 = mybir.dt.float32
BF16 = mybir.dt.bfloat16
AF = mybir.ActivationFunctionType
ALU = mybir.AluOpType
AX = mybir.AxisListType

D = 1024
INW = 1792
QW = 512
CW = 512
HD = 64
CONV_K = 31
PAD = 15
EPS = 1e-6
LN_EPS = 1e-5
N_CORES = 8

ENGS = ("pe", "act", "dve", "pool", "sp")


def I(method, *args, **kw):
    return (method, args, kw)


class Buf:
    __slots__ = ("name", "w", "r")

    def __init__(self, name):
        self.name = name
        self.w = None
        self.r = []


class Sched:
    def __init__(self, nc, n_dma_sems=24):
        self.nc = nc
        self.ops = {e: [] for e in ENGS}
        self.sem_names = list(ENGS[:4]) + ["d%d" % i for i in range(n_dma_sems)]
        self.count = {s: 0 for s in self.sem_names}
        self.seen = {e: {s: 0 for s in self.sem_names} for e in ENGS}
        self.dma_rr = {}
        self.dma_pool = {"sp": (0, n_dma_sems * 2 // 3), "pool": (n_dma_sems * 2 // 3, n_dma_sems)}
        self.n_instr = {e: 0 for e in ENGS}
        self.n_wait = {e: 0 for e in ENGS}

    def _waits_for(self, eng, reads, writes):
        need = {}
        seen = self.seen[eng]

        def add(tok):
            s, v = tok
            if s == "pe" and eng == "pe":
                return
            if seen[s] < v and need.get(s, 0) < v:
                need[s] = v

        for b in reads:
            if b.w is not None:
                add(b.w)
        for b in writes:
            if b.w is not None:
                add(b.w)
            for t in b.r:
                add(t)
        for s, v in need.items():
            seen[s] = v
        return list(need.items())

    def _commit(self, tok, reads, writes):
        for b in reads:
            s = tok[0]
            b.r = [t for t in b.r if t[0] != s]
            b.r.append(tok)
        for b in writes:
            b.w = tok
            b.r = []

    def op(self, eng, fn, reads=(), writes=()):
        waits = self._waits_for(eng, reads, writes)
        self.count[eng] += 1
        tok = (eng, self.count[eng])
        self.ops[eng].append((waits, fn, (eng, 1)))
        self.n_instr[eng] += 1
        self.n_wait[eng] += len(waits)
        self._commit(tok, reads, writes)
        return tok

    def group(self, eng, fns, reads=(), writes=()):
        waits = self._waits_for(eng, reads, writes)
        self.count[eng] += 1
        tok = (eng, self.count[eng])
        n = len(fns)
        for i, fn in enumerate(fns):
            self.ops[eng].append((waits if i == 0 else [], fn, (eng, 1) if i == n - 1 else None))
        self.n_instr[eng] += n
        self.n_wait[eng] += len(waits)
        self._commit(tok, reads, writes)
        return tok

    def dma(self, q, fn, reads=(), writes=()):
        waits = self._waits_for(q, reads, writes)
        lo, hi = self.dma_pool[q]
        i = self.dma_rr.get(q, lo)
        sem = "d%d" % i
        self.dma_rr[q] = lo + ((i + 1 - lo) % (hi - lo))
        prev = self.count[sem]
        if prev > self.seen[q][sem]:
            self.seen[q][sem] = prev
            waits = [w for w in waits if w[0] != sem] + [(sem, prev)]
        self.count[sem] += 16
        tok = (sem, self.count[sem])
        self.ops[q].append((waits, fn, (sem, 16)))
        self.n_instr[q] += 1
        self.n_wait[q] += len(waits)
        self._commit(tok, reads, writes)
        return tok

    def final_wait(self, eng, bufs):
        waits = self._waits_for(eng, bufs, ())
        self.ops[eng].append((waits, None, None))

    def emit(self):
        nc = self.nc
        with contextlib.ExitStack() as st:
            sems = {s: st.enter_context(nc.semaphore("s_" + s)) for s in self.sem_names}
            block = st.enter_context(nc.Block())

            def run(e):
                def body(engh):
                    for waits, fn, inc in self.ops[e]:
                        for (s, v) in waits:
                            engh.wait_ge(sems[s], v)
                        if fn is None:
                            continue
                        ins = getattr(engh, fn[0])(*fn[1], **fn[2])
                        if inc is not None:
                            ins.then_inc(sems[inc[0]], inc[1])
                return body

            block.tensor(run("pe"))
            block.scalar(run("act"))
            block.vector(run("dve"))
            block.gpsimd(run("pool"))
            block.sync(run("sp"))


def default_cfg():
    return dict(NS=4, LS=2048, LP=8192, LQ=4096, QJ=1024, GT=512, DFF=4096)


def build_program(cfg):
    NS, LS, LP, LQ, QJ, GT, DFF = (cfg[k] for k in ("NS", "LS", "LP", "LQ", "QJ", "GT", "DFF"))
    NFF = DFF // 128
    NTJ = QJ // 128
    NGJ = QJ // GT
    TPG = GT // 128
    LMAX = max(LP, LS)
    NKC_MAX = LMAX // 128

    nc = bass.Bass("TRN2", target_bir_lowering=False)

    def din(name, shape, dt=F32):
        return nc.dram_tensor(name, list(shape), dt, kind="ExternalInput").ap()

    xs = din("xs", [NS * LS, D])
    xp = din("xp", [LP, D])
    xq = din("xq", [LQ + 256, D])
    tab = din("tab", [LMAX, 64])
    tabq = din("tabq", [LQ, 64])
    w_in_h = din("w_in_h", [128, 8, INW])
    w_out_h = din("w_out_h", [128, 8, D])
    w_up_h = din("w_up_h", [NFF, 128, 8, 128])
    w_down_h = din("w_down_h", [NFF, 128, D])
    gmix_h = din("gmix", [128, 8])
    gmlp_h = din("gmlp", [128, 8])
    gqk_h = din("gqk", [128, 640])
    gfin_h = din("gfin", [128, D])
    convw_h = din("convw", [128, 4, CONV_K])
    convb_h = din("convb", [128, 4])
    lng_h = din("lng", [128, 4])
    lnb_h = din("lnb", [128, 4])
    ident_h = din("ident", [128, 128], BF16)
    onesd_h = din("onesd", [128, 128], BF16)
    ys = nc.dram_tensor("ys", [NS * LS, D], F32, kind="ExternalOutput").ap()
    yp = nc.dram_tensor("yp", [LQ, D], F32, kind="ExternalOutput").ap()
    wus = nc.dram_tensor("wus", [NFF, 128, 1024], BF16, kind="Internal").ap()
    wds = nc.dram_tensor("wds", [NFF, 128, 1024], BF16, kind="Internal").ap()

    S = Sched(nc)
    st = contextlib.ExitStack()

    def sb(name, shape, dt):
        return st.enter_context(nc.sbuf_tensor(name, list(shape), dt))

    w_in_bf = sb("w_in_bf", [128, 8, INW], BF16)
    w_out_bf = sb("w_out_bf", [128, 8, D], BF16)
    ident = sb("ident_sb", [128, 128], BF16)
    onesd = sb("onesd_sb", [128, 128], BF16)
    gqk = sb("gqk_sb", [128, 640], F32)
    gfin = sb("gfin_sb", [128, D], F32)
    gmix = sb("gmix_sb", [128, 8], F32)
    gmlp = sb("gmlp_sb", [128, 8], F32)
    convw = sb("convw_sb", [128, 4, CONV_K], F32)
    convb = sb("convb_sb", [128, 4], F32)
    lng = sb("lng_sb", [128, 4], F32)
    lnb = sb("lnb_sb", [128, 4], F32)
    KT = sb("KT", [128, LMAX], BF16)
    V = sb("V", [128, NKC_MAX, 192], BF16)
    QT = sb("QT", [128, 4, QJ], BF16)
    CA = sb("CA", [128, 4, QJ], BF16)
    NPT = 3
    PT = [sb("PT%d" % i, [128, 1024], BF16) for i in range(NPT)]
    xt = [sb("xt%d" % i, [128, D], F32) for i in range(2)]
    tabt = [sb("tabt%d" % i, [128, 64], F32) for i in range(5)]
    xnb = [sb("xnb%d" % i, [128, D], BF16) for i in range(2)]
    xnT = sb("xnT", [128, 8, GT], BF16)
    ssq = [sb("ssq%d" % i, [128, 1], F32) for i in range(2)]
    rstd = [sb("rstd%d" % i, [128, 1], F32) for i in range(2)]
    ssq2 = [sb("ssq2_%d" % i, [128, 8], F32) for i in range(2)]
    rstd2 = [sb("rstd2_%d" % i, [128, 8], F32) for i in range(2)]
    rd = sb("rd", [128, 512], F32)
    dg = [sb("dg%d" % i, [128, 128], BF16) for i in range(4)]
    scr1 = sb("scr1", [128, 8], F32)
    rs2 = sb("rs2", [128, 8], F32)

    P1_BYTES = (640 * 4) * 4 + 320 * 4 * 2 + 640 * 2 * 2 + 4 * (QJ + 2 * PAD) * 4 + 4 * QJ * 4 + 4 * GT * 2 + GT * 4 * 3 + 2 * 2048 + 8 * GT * 2
    P3_BYTES = NFF * GT * 2 + 8 * 2048 + TPG * D * 4 + GT * 4 * 2
    PRO_BYTES = P1_BYTES + 2 * 2048 * 4 + 2 * 2048 * 2 if (4 * (QJ + 2 * PAD) * 4 + 4 * QJ * 4) < 24576 else 0
    R_BYTES = max(P1_BYTES, P3_BYTES, PRO_BYTES)
    R = sb("R", [128, R_BYTES // 4 + 8], F32)

    class Carver:
        def __init__(self):
            self.off = 0

        def take(self, nelem, dt):
            nbytes = nelem * (4 if dt == F32 else 2)
            nw = (nbytes + 3) // 4
            a = R[:, self.off:self.off + nw]
            self.off += nw
            assert self.off * 4 <= R_BYTES + 32, (self.off * 4, R_BYTES)
            return a if dt == F32 else a.bitcast(BF16)

    c1 = Carver()
    sq = [c1.take(640, F32) for _ in range(2)]
    zn = [c1.take(640, F32) for _ in range(2)]
    t1 = c1.take(320, F32)
    t2 = c1.take(320, F32)
    qkb = [c1.take(640, BF16) for _ in range(2)]
    cbuf_off = c1.off
    cbuf_flat = c1.take(4 * (QJ + 2 * PAD), F32)
    cbuf = cbuf_flat.rearrange("p (c t) -> p c t", c=4)
    cbufb = cbuf_flat.bitcast(BF16)[:, 0:4 * (QJ + 2 * PAD)].rearrange("p (c t) -> p c t", c=4)
    co = c1.take(4 * QJ, F32).rearrange("p (c t) -> p c t", c=4)
    cobf = c1.take(4 * GT, BF16).rearrange("p (c t) -> p c t", c=4)
    sig2 = c1.take(2 * GT, F32)
    sig = [sig2[:, 0:GT], sig2[:, GT:2 * GT]]
    assert QJ <= 2 * GT
    ctmp = sig2[:, 0:QJ]
    lnr = c1.take(GT, F32)
    xnTh = [c1.take(8 * 128, BF16).rearrange("p (k t) -> p k t", k=8) for _ in range(2)]
    xnT1 = c1.take(8 * GT, BF16).rearrange("p (k t) -> p k t", k=8)
    xnTg = [xnT, xnT1]

    c3 = Carver()
    u2T = c3.take(NFF * GT, BF16).rearrange("p (f t) -> p f t", f=NFF)
    wpool = [c3.take(1024, BF16) for _ in range(8)]
    hbuf = c3.take(TPG * D, F32).rearrange("p (t d) -> p t d", t=TPG)
    relu2 = c3.take(2 * GT, F32)
    relu = [relu2[:, 0:GT], relu2[:, GT:2 * GT]]

    c0 = Carver()
    c0.off = cbuf_off if (4 * (QJ + 2 * PAD) * 4 + 4 * QJ * 4) >= 24576 else c1.off
    stg = [c0.take(2048, F32) for _ in range(2)]
    stgb = [c0.take(2048, BF16) for _ in range(2)]

    ps = st.enter_context(nc.psum_tensor("ps", [128, 4096], F32))

    def bank(b, n=1):
        return ps[:, b * 512:(b + n) * 512]

    def bank_bf(b):
        return ps[:, b * 512:(b + 1) * 512].bitcast(BF16)

    PB = [Buf("pb%d" % i) for i in range(8)]
    RG = Buf("RG")
    B_win, B_wout, B_const = Buf("win"), Buf("wout"), Buf("const")
    B_KT = [Buf("KT%d" % i) for i in range(NKC_MAX)]
    B_V = [Buf("V%d" % i) for i in range(NKC_MAX)]
    B_QT = [[Buf("QT%d_%d" % (c, g)) for g in range(NGJ)] for c in range(4)]
    B_CA = [Buf("CA%d" % g) for g in range(NGJ)]
    B_PT = [Buf("PT%d" % i) for i in range(NPT)]
    B_xt = [Buf("xt%d" % i) for i in range(2)]
    B_tabt = [Buf("tabt%d" % i) for i in range(5)]
    B_xnb = [Buf("xnb%d" % i) for i in range(2)]
    B_xnT = Buf("xnT")
    B_xnTh = [Buf("xnTh%d" % i) for i in range(2)]
    B_xnTg = [B_xnT, Buf("xnT1")]
    B_ctmp = Buf("ctmp")
    B_st = [Buf("st%d" % i) for i in range(2)]
    B_st2 = [Buf("st2_%d" % i) for i in range(2)]
    B_rd = Buf("rd")
    B_dg = [Buf("dg%d" % i) for i in range(4)]
    B_sq, B_zn, B_qkb = [Buf("sq0"), Buf("sq1")], [Buf("zn0"), Buf("zn1")], [Buf("qkb0"), Buf("qkb1")]
    B_t1, B_t2 = Buf("t1"), Buf("t2")
    B_cbuf = [Buf("cbuf%d" % c) for c in range(4)]
    B_co = [Buf("co%d" % c) for c in range(4)]
    B_cobf, B_sig, B_lnr = Buf("cobf"), [Buf("sig0"), Buf("sig1")], Buf("lnr")
    B_u2T = [Buf("u2T%d" % f) for f in range(NFF)]
    B_wpool = [Buf("wp%d" % i) for i in range(8)]
    B_h = [Buf("h%d" % i) for i in range(TPG)]
    B_relu = [Buf("relu0"), Buf("relu1")]
    B_stg = [Buf("stg0"), Buf("stg1")]
    B_stgb = [Buf("stgb0"), Buf("stgb1")]
    B_wus = [Buf("wus%d" % f) for f in range(NFF)]
    B_wds = [Buf("wds%d" % f) for f in range(NFF)]
    B_out = [Buf("out%d" % i) for i in range(8)]
    octr = {"n": 0}
    B_scr = Buf("scr1")
    B_rs2 = [Buf("rs2_%d" % i) for i in range(8)]

    def region_switch():
        S.op("pool", I("memset", scr1[:], 0.0), reads=[], writes=[RG, B_scr])

    for dst, src in ((ident, ident_h), (onesd, onesd_h), (gqk, gqk_h), (gfin, gfin_h), (gmix, gmix_h),
                     (gmlp, gmlp_h), (convw, convw_h), (convb, convb_h), (lng, lng_h), (lnb, lnb_h)):
        S.dma("sp", I("dma_start", out=dst[:], in_=src), writes=[B_const])
    S.op("pool", I("memset", V[:, :, 64:128], 1.0), writes=B_V)

    sctr = {"n": 0}

    def nxt():
        j = sctr["n"] % 2
        sctr["n"] += 1
        return j

    for k in range(8):
        j = nxt()
        S.dma("sp", I("dma_start", out=stg[j][:, 0:INW], in_=w_in_h[:, k, :]), reads=[RG], writes=[B_stg[j]])
        S.op("dve", I("tensor_scalar", out=w_in_bf[:, k, :], in0=stg[j][:, 0:INW], scalar1=gmix[:, k:k + 1], scalar2=None,
                      op0=ALU.mult), reads=[RG, B_stg[j], B_const], writes=[B_win])
    for k in range(0, 8, 2):
        j = nxt()
        S.dma("sp", I("dma_start", out=stg[j][:, 0:2048].rearrange("p (k c) -> p k c", k=2), in_=w_out_h[:, k:k + 2, :]),
              reads=[RG], writes=[B_stg[j]])
        S.op("act", I("copy", out=w_out_bf[:, k:k + 2, :].rearrange("p k c -> p (k c)"), in_=stg[j][:, 0:2048]),
             reads=[RG, B_stg[j]], writes=[B_wout])

    FB = 2
    pro_thunks = []
    pend = {"out": None}

    def flush_out():
        if pend["out"] is not None:
            pend["out"]()
            pend["out"] = None

    for f0 in range(0, NFF, FB):
        def cv_up(f0=f0):
            j = nxt()
            S.dma("sp", I("dma_start", out=stg[j][:, 0:FB * 1024].rearrange("p (f n) -> p f n", f=FB),
                          in_=w_up_h[f0:f0 + FB].rearrange("f p k c -> p f (k c)")), reads=[RG], writes=[B_stg[j]])
            S.op("dve", I("tensor_tensor", out=stgb[j][:, 0:FB * 1024].rearrange("p (f k c) -> p f k c", f=FB, k=8),
                          in0=stg[j][:, 0:FB * 1024].rearrange("p (f k c) -> p f k c", f=FB, k=8),
                          in1=gmlp[:].unsqueeze(1).unsqueeze(3).to_broadcast([128, FB, 8, 128]), op=ALU.mult),
                 reads=[RG, B_stg[j], B_const], writes=[B_stgb[j]])
            flush_out()
            pend["out"] = lambda: S.dma("sp", I("dma_start", out=wus[f0:f0 + FB].rearrange("f p n -> p f n"),
                                                in_=stgb[j][:, 0:FB * 1024].rearrange("p (f n) -> p f n", f=FB)),
                                        reads=[RG, B_stgb[j]], writes=B_wus[f0:f0 + FB])
        pro_thunks.append(cv_up)
    for f0 in range(0, NFF, FB):
        def cv_dn(f0=f0):
            j = nxt()
            S.dma("sp", I("dma_start", out=stg[j][:, 0:FB * 1024].rearrange("p (f n) -> p f n", f=FB),
                          in_=w_down_h[f0:f0 + FB].rearrange("f p n -> p f n")), reads=[RG], writes=[B_stg[j]])
            S.op("act", I("copy", out=stgb[j][:, 0:FB * 1024], in_=stg[j][:, 0:FB * 1024]), reads=[RG, B_stg[j]], writes=[B_stgb[j]])
            flush_out()
            pend["out"] = lambda: S.dma("sp", I("dma_start", out=wds[f0:f0 + FB].rearrange("f p n -> p f n"),
                                                in_=stgb[j][:, 0:FB * 1024].rearrange("p (f n) -> p f n", f=FB)),
                                        reads=[RG, B_stgb[j]], writes=B_wds[f0:f0 + FB])
        pro_thunks.append(cv_dn)
    pro_thunks.append(flush_out)

    state = {"slot": 0}

    def pipeline(items, stages):
        ns = len(stages)
        for step in range(len(items) + ns - 1):
            for s_ in range(ns):
                i = step - s_
                if 0 <= i < len(items):
                    stages[s_](i, items[i])

    NTAB = len(tabt)

    def fe_a(i, x_ap, tab_ap):
        sl = i % 2
        S.dma("sp", I("dma_start", out=xt[sl][:], in_=x_ap), writes=[B_xt[sl]])
        if tab_ap is not None:
            S.dma("sp", I("dma_start", out=tabt[i % NTAB][:], in_=tab_ap), writes=[B_tabt[i % NTAB]])
        S.op("act", I("activation", out=xnb[sl][:], in_=xt[sl][:], func=AF.Square, accum_out=ssq[sl][:]),
             reads=[B_xt[sl]], writes=[B_xnb[sl], B_st[sl]])
        S.op("act", I("activation", out=rstd[sl][:], in_=ssq[sl][:], func=AF.Sqrt, bias=EPS, scale=1.0 / D),
             reads=[B_st[sl]], writes=[B_st[sl]])
        S.op("dve", I("reciprocal", out=rstd[sl][:], in_=rstd[sl][:]), reads=[B_st[sl]], writes=[B_st[sl]])
        S.op("dve", I("tensor_scalar", out=xnb[sl][:], in0=xt[sl][:], scalar1=rstd[sl][:, 0:1], scalar2=None, op0=ALU.mult),
             reads=[B_xt[sl], B_st[sl]], writes=[B_xnb[sl]])

    def fe_b(i, dstT, dstT_buf, cols):
        sl = i % 2
        tb = 2 + sl
        pv = bank_bf(tb)
        S.group("pe", [I("transpose", out=pv[:, k * 128:(k + 1) * 128], in_=xnb[sl][:, k * 128:(k + 1) * 128],
                         identity=ident[:]) for k in range(8)], reads=[B_xnb[sl], B_const], writes=[PB[tb]])
        S.op("dve", I("tensor_copy", out=dstT[:, :, cols], in_=pv.rearrange("p (k t) -> p k t", k=8)),
             reads=[RG, PB[tb]], writes=[dstT_buf])

    def qk_sq(i, zps, zbuf, nh):
        sl = i % 2
        S.op("act", I("activation", out=sq[sl][:, 0:nh * 64], in_=zps, func=AF.Square), reads=[RG, zbuf], writes=[B_sq[sl]])

    def qk_norm(i, zps, zbuf, nh, goff):
        sl = i % 2
        w = nh * 64
        S.op("dve", I("tensor_reduce", out=ssq2[sl][:, 0:nh], in_=sq[sl][:, 0:w].rearrange("p (h d) -> p h d", h=nh),
                      axis=AX.X, op=ALU.add), reads=[RG, B_sq[sl]], writes=[B_st2[sl]])
        S.op("act", I("activation", out=rstd2[sl][:, 0:nh], in_=ssq2[sl][:, 0:nh], func=AF.Sqrt, bias=EPS, scale=1.0 / HD),
             reads=[B_st2[sl]], writes=[B_st2[sl]])
        S.op("dve", I("reciprocal", out=rstd2[sl][:, 0:nh], in_=rstd2[sl][:, 0:nh]), reads=[B_st2[sl]], writes=[B_st2[sl]])
        S.op("dve", I("tensor_tensor", out=zn[sl][:, 0:w].rearrange("p (h d) -> p h d", h=nh),
                      in0=zps.rearrange("p (h d) -> p h d", h=nh),
                      in1=rstd2[sl][:, 0:nh].unsqueeze(2).to_broadcast([128, nh, HD]), op=ALU.mult),
             reads=[RG, zbuf, B_st2[sl]], writes=[B_zn[sl]])
        S.op("dve", I("tensor_tensor", out=zn[sl][:, 0:w], in0=zn[sl][:, 0:w], in1=gqk[:, goff:goff + w], op=ALU.mult),
             reads=[RG, B_zn[sl], B_const], writes=[B_zn[sl]])

    def rope(i, nh, re="pool"):
        sl = i % 2
        w = nh * 64
        tb_ = tabt[i % NTAB]
        btab = B_tabt[i % NTAB]
        zv = zn[sl][:, 0:w].rearrange("p (h a f i) -> p h a f i", h=nh, a=2, f=2)
        ov = qkb[sl][:, 0:w].rearrange("p (h a f i) -> p h a f i", h=nh, a=2, f=2)
        x1, x2 = zv[:, :, :, 0, :], zv[:, :, :, 1, :]
        o1, o2 = ov[:, :, :, 0, :], ov[:, :, :, 1, :]
        cosb = tb_[:, 0:32].rearrange("p (a i) -> p a i", a=2).unsqueeze(1).to_broadcast([128, nh, 2, 16])
        sinb = tb_[:, 32:64].rearrange("p (a i) -> p a i", a=2).unsqueeze(1).to_broadcast([128, nh, 2, 16])
        hw = nh * 32
        t1v = t1[:, 0:hw].rearrange("p (h a i) -> p h a i", h=nh, a=2)
        t2v = t2[:, 0:hw].rearrange("p (h a i) -> p h a i", h=nh, a=2)
        S.op(re, I("tensor_tensor", out=t1v, in0=x1, in1=cosb, op=ALU.mult), reads=[RG, B_zn[sl], btab], writes=[B_t1])
        S.op(re, I("tensor_tensor", out=t2v, in0=x2, in1=sinb, op=ALU.mult), reads=[RG, B_zn[sl], btab], writes=[B_t2])
        S.op(re, I("tensor_tensor", out=o1, in0=t1v, in1=t2v, op=ALU.subtract), reads=[RG, B_t1, B_t2], writes=[B_qkb[sl]])
        S.op(re, I("tensor_tensor", out=t1v, in0=x2, in1=cosb, op=ALU.mult), reads=[RG, B_zn[sl], btab], writes=[B_t1])
        S.op(re, I("tensor_tensor", out=t2v, in0=x1, in1=sinb, op=ALU.mult), reads=[RG, B_zn[sl], btab], writes=[B_t2])
        S.op(re, I("tensor_tensor", out=o2, in0=t1v, in1=t2v, op=ALU.add), reads=[RG, B_t1, B_t2], writes=[B_qkb[sl]])

    def ctx_stages(x_src, nt, tab_src, side=None):
        def stA1(i, it):
            t = it[1]
            fe_a(i, x_src(t), tab_src(t))
            if side:
                side.pop(0)()

        def stA2(i, it):
            fe_b(i, xnTh[i % 2], B_xnTh[i % 2], slice(0, 128))

        def stB1a(i, it):
            t = it[1]
            kb = i % 2
            S.group("pe", [I("matmul", bank(kb)[:, 0:256], lhsT=xnTh[i % 2][:, k, :], rhs=w_in_bf[:, k, 512:768],
                             start=(k == 0), stop=(k == 7)) for k in range(8)],
                    reads=[RG, B_xnTh[i % 2], B_win], writes=[PB[kb]])
            S.op("act", I("copy", out=V[:, t, :].rearrange("p (a d) -> p a d", a=3)[:, 0:3:2, :],
                          in_=bank(kb)[:, 128:256].rearrange("p (a d) -> p a d", a=2)), reads=[PB[kb]], writes=[B_V[t]])
            qk_sq(i, bank(kb)[:, 0:128], PB[kb], 2)

        def stB1b(i, it):
            kb = i % 2
            qk_norm(i, bank(kb)[:, 0:128], PB[kb], 2, 512)

        def stB2a(i, it):
            rope(i, 2)

        def stB2b(i, it):
            t = it[1]
            tb = 4 + i % 2
            pv = bank_bf(tb)
            S.op("pe", I("transpose", out=pv[:, 0:128], in_=qkb[i % 2][:, 0:128], identity=ident[:]),
                 reads=[RG, B_qkb[i % 2], B_const], writes=[PB[tb]])
            S.op("dve", I("tensor_copy", out=KT[:, t * 128:(t + 1) * 128], in_=pv[:, 0:128]), reads=[PB[tb]], writes=[B_KT[t]])

        return [("C", t) for t in range(nt)], [stA1, stA2, stB1a, stB1b, stB2a, stB2b]

    def glu_chunks(rhsT, rbuf, n, dst_col0, cdst):
        for c in range(4):
            bv, bg = (5 if c % 2 == 0 else 7), 6
            S.group("pe", [I("matmul", bank(bv)[:, 0:n], lhsT=w_in_bf[:, k, 768 + c * 128:768 + (c + 1) * 128],
                             rhs=rhsT[:, k, :], start=(k == 0), stop=(k == 7)) for k in range(8)],
                    reads=[RG, rbuf, B_win], writes=[PB[bv]])
            S.group("pe", [I("matmul", bank(bg)[:, 0:n], lhsT=w_in_bf[:, k, 1280 + c * 128:1280 + (c + 1) * 128],
                             rhs=rhsT[:, k, :], start=(k == 0), stop=(k == 7)) for k in range(8)],
                    reads=[RG, rbuf, B_win], writes=[PB[bg]])
            S.op("act", I("activation", out=sig[c % 2][:, 0:n], in_=bank(bg)[:, 0:n], func=AF.Sigmoid),
                 reads=[RG, PB[bg]], writes=[B_sig[c % 2]])
            S.op("dve", I("tensor_tensor", out=cdst[:, c, dst_col0:dst_col0 + n], in0=bank(bv)[:, 0:n],
                          in1=sig[c % 2][:, 0:n], op=ALU.mult), reads=[RG, PB[bv], B_sig[c % 2]], writes=[B_cbuf[c]])

    def job_stages(x_src, tab_src, has_left, has_right, cdst):
        items = []
        if has_left:
            items.append(("L", -1))
        else:
            S.op("pool", I("memset", cdst[:, :, 0:PAD], 0.0), reads=[RG], writes=B_cbuf)
        items += [("B", t) for t in range(NTJ)]
        if has_right:
            items.append(("R", NTJ))
        else:
            S.op("pool", I("memset", cdst[:, :, PAD + QJ:PAD + QJ + PAD], 0.0), reads=[RG], writes=B_cbuf)

        def stA1(i, it):
            kind, t = it
            fe_a(i, x_src(t), tab_src(t) if kind == "B" else None)

        def stA2(i, it):
            kind, t = it
            if kind == "B":
                g, tt = t // TPG, t % TPG
                fe_b(i, xnTg[g % 2], B_xnTg[g % 2], slice(tt * 128, (tt + 1) * 128))
            else:
                fe_b(i, xnTh[i % 2], B_xnTh[i % 2], slice(0, 128))

        def stB1a(i, it):
            kind, t = it
            if kind == "L":
                glu_chunks(xnTh[i % 2][:, :, 128 - PAD:128], B_xnTh[i % 2], PAD, 0, cdst)
            elif kind == "R":
                glu_chunks(xnTh[i % 2][:, :, 0:PAD], B_xnTh[i % 2], PAD, PAD + QJ, cdst)
            else:
                g, tt = t // TPG, t % TPG
                qb = i % 2
                S.group("pe", [I("matmul", bank(qb), lhsT=xnTg[g % 2][:, k, tt * 128:(tt + 1) * 128], rhs=w_in_bf[:, k, 0:512],
                                 start=(k == 0), stop=(k == 7)) for k in range(8)],
                        reads=[RG, B_xnTg[g % 2], B_win], writes=[PB[qb]])
                qk_sq(i, bank(qb), PB[qb], 8)
                if tt == TPG - 1:
                    glu_chunks(xnTg[g % 2][:, :, :], B_xnTg[g % 2], GT, PAD + g * GT, cdst)

        def stB1b(i, it):
            kind, t = it
            if kind == "B":
                qk_norm(i, bank(i % 2), PB[i % 2], 8, 0)

        def stB2a(i, it):
            if it[0] == "B":
                rope(i, 8)

        def stB2b(i, it):
            kind, t = it
            if kind != "B":
                return
            g = t // TPG
            pv = bank_bf(4)
            S.group("pe", [I("transpose", out=pv[:, c * 128:(c + 1) * 128], in_=qkb[i % 2][:, c * 128:(c + 1) * 128],
                             identity=ident[:]) for c in range(4)], reads=[RG, B_qkb[i % 2], B_const], writes=[PB[4]])
            S.op("act", I("copy", out=QT[:, :, t * 128:(t + 1) * 128], in_=pv[:, 0:512].rearrange("p (c t) -> p c t", c=4)),
                 reads=[PB[4]], writes=[B_QT[c][g] for c in range(4)])

        return items, [stA1, stA2, stB1a, stB1b, stB2a, stB2b]

    def conv_ln_thunks(do_conv=True):
        th = []
        POOL_CHUNKS = cfg.get("pool_chunks", ())
        per_chunk = []
        for c in (range(4) if do_conv else []):
            lst = []
            if c in POOL_CHUNKS:
                lst.append(lambda fb, c=c: S.op("pool", I("tensor_scalar", out=co[:, c, :], in0=cbuf[:, c, 0:QJ],
                                                          scalar1=convw[:, c, 0:1], scalar2=convb[:, c:c + 1], op0=ALU.mult, op1=ALU.add),
                                                reads=[RG, B_cbuf[c], B_const], writes=[B_co[c]]))
                for j in range(1, CONV_K):
                    def tap(fb, c=c, j=j):
                        S.op("pool", I("tensor_scalar", out=ctmp[:], in0=cbuf[:, c, j:j + QJ], scalar1=convw[:, c, j:j + 1],
                                       scalar2=None, op0=ALU.mult), reads=[RG, B_cbuf[c], B_const], writes=B_sig)
                        S.op("pool", I("tensor_tensor", out=co[:, c, :], in0=co[:, c, :], in1=ctmp[:], op=ALU.add),
                             reads=[RG, B_co[c]] + B_sig, writes=[B_co[c]])
                    lst.append(tap)
            else:
                lst.append(lambda fb, c=c: S.op("dve", I("tensor_scalar", out=co[:, c, :], in0=cbuf[:, c, 0:QJ],
                                                         scalar1=convw[:, c, 0:1], scalar2=convb[:, c:c + 1], op0=ALU.mult, op1=ALU.add),
                                                reads=[RG, B_cbuf[c], B_const], writes=[B_co[c]]))
                for j in range(1, CONV_K):
                    lst.append(lambda fb, c=c, j=j: S.op("dve", I("scalar_tensor_tensor", out=co[:, c, :], in0=cbuf[:, c, j:j + QJ],
                                                                 scalar=convw[:, c, j:j + 1], in1=co[:, c, :], op0=ALU.mult, op1=ALU.add),
                                                         reads=[RG, B_cbuf[c], B_const, B_co[c]], writes=[B_co[c]]))
            per_chunk.append(lst)
        dve_l = [x for c in range(len(per_chunk)) if c not in POOL_CHUNKS for x in per_chunk[c]]
        pool_l = [x for c in range(len(per_chunk)) if c in POOL_CHUNKS for x in per_chunk[c]]
        n_d, n_p = len(dve_l), len(pool_l)
        pi = 0
        for di, x in enumerate(dve_l):
            th.append(x)
            while pi < n_p and (pi + 1) * n_d <= (di + 1) * n_p:
                th.append(pool_l[pi])
                pi += 1
        th += pool_l[pi:]
        for g in range(NGJ if cfg.get("noln") is None else 0):
            gs = slice(g * GT, (g + 1) * GT)
            th.append(lambda fb, gs=gs: S.op("dve", I("tensor_copy", out=cobf[:], in_=co[:, :, gs]),
                                         reads=[RG] + B_co, writes=[B_cobf]))
            def mean_sub(fb, gs=gs):
                S.group("pe", [I("matmul", bank(fb)[:, 0:GT], lhsT=onesd[:], rhs=cobf[:, c, :],
                               start=(c == 0), stop=(c == 3)) for c in range(4)],
                        reads=[RG, B_cobf, B_const], writes=[PB[fb]])
                S.op("dve", I("tensor_tensor", out=co[:, :, gs], in0=co[:, :, gs],
                              in1=bank(fb)[:, 0:GT].unsqueeze(1).to_broadcast([128, 4, GT]), op=ALU.subtract),
                     reads=[RG, PB[fb]] + B_co, writes=B_co)
            th.append(mean_sub)
            th.append(lambda fb, gs=gs: S.op("act", I("activation", out=cobf[:], in_=co[:, :, gs], func=AF.Square),
                                         reads=[RG] + B_co, writes=[B_cobf]))
            def var_sqrt(fb):
                S.group("pe", [I("matmul", bank(fb)[:, 0:GT], lhsT=onesd[:], rhs=cobf[:, c, :],
                               start=(c == 0), stop=(c == 3)) for c in range(4)],
                        reads=[RG, B_cobf, B_const], writes=[PB[fb]])
                S.op("act", I("activation", out=lnr[:], in_=bank(fb)[:, 0:GT], func=AF.Sqrt, bias=LN_EPS, scale=1.0),
                     reads=[RG, PB[fb]], writes=[B_lnr])
            th.append(var_sqrt)
            th.append(lambda fb: S.op("dve", I("reciprocal", out=lnr[:], in_=lnr[:]), reads=[RG, B_lnr], writes=[B_lnr]))
            th.append(lambda fb, gs=gs: S.op("dve", I("tensor_tensor",
                out=co[:, :, gs], in0=co[:, :, gs], in1=lnr[:].unsqueeze(1).to_broadcast([128, 4, GT]), op=ALU.mult),
                reads=[RG, B_lnr] + B_co, writes=B_co))
            for c in range(4):
                th.append(lambda fb, c=c, gs=gs, g=g: S.op("act", I("activation",
                    out=CA[:, c, gs], in_=co[:, c, gs], func=AF.Silu, scale=lng[:, c:c + 1], bias=lnb[:, c:c + 1]),
                    reads=[RG, B_co[c], B_const], writes=[B_CA[g]]))
        if cfg.get('lnsteps') is not None:
            th = th[:4 * CONV_K + cfg['lnsteps']]
        return th

    def conv_pe():
        dctr = 0
        for c in range(4):
            b0 = 2 * (c % 2)
            for j in range(CONV_K):
                sl = dctr % 4
                dctr += 1
                S.op("act", I("activation", out=dg[sl][:], in_=ident[:], func=AF.Identity, scale=convw[:, c, j:j + 1]),
                     reads=[B_const], writes=[B_dg[sl]])
                S.group("pe", [I("matmul", bank(b0 + g)[:, 0:GT], lhsT=dg[sl][:], rhs=cbufb[:, c, g * GT + j:g * GT + j + GT],
                                 start=(j == 0), stop=(j == CONV_K - 1)) for g in range(NGJ)],
                        reads=[RG, B_dg[sl], B_cbuf[c]], writes=[PB[b0 + g] for g in range(NGJ)])
            for g in range(NGJ):
                S.op("dve", I("tensor_scalar", out=co[:, c, g * GT:(g + 1) * GT], in0=bank(b0 + g)[:, 0:GT],
                              scalar1=convb[:, c:c + 1], scalar2=None, op0=ALU.add),
                     reads=[RG, PB[b0 + g], B_const], writes=[B_co[c]])

    def attn_phase(nkc, side):
        its = [(hp, g, kc) for hp in range(4) for g in range(NGJ) for kc in range(nkc)]
        n_it = len(its)
        n_side = len(side)
        emitted = 0

        def emit_S(i):
            hp, g, kc = its[i]
            qs = slice(g * GT, (g + 1) * GT)
            ks = slice(kc * 128, (kc + 1) * 128)
            sb_ = 2 * (i % 2)
            S.group("pe", [
                I("matmul", bank(sb_)[:, 0:GT], lhsT=KT[0:64, ks], rhs=QT[0:64, hp, qs], start=True, stop=True,
                  tile_position=(0, 0)),
                I("matmul", bank(sb_ + 1)[:, 0:GT], lhsT=KT[64:128, ks], rhs=QT[64:128, hp, qs], start=True, stop=True,
                  tile_position=(64, 0)),
            ], reads=[B_KT[kc], B_QT[hp][g]], writes=[PB[sb_], PB[sb_ + 1]])

        for i0 in range(min(2, n_it)):
            emit_S(i0)
        for i in range(n_it):
            hp, g, kc = its[i]
            qs = slice(g * GT, (g + 1) * GT)
            ob = 4 + 2 * ((hp * NGJ + g) % 2)
            oA, oB = bank(ob), bank(ob + 1)
            sb_ = 2 * (i % 2)
            pt = PT[i % NPT]
            bpt = B_PT[i % NPT]
            if GT == 512:
                S.op("act", I("activation", out=pt[:], in_=bank(sb_, 2), func=AF.Exp, scale=0.125),
                     reads=[PB[sb_], PB[sb_ + 1]], writes=[bpt])
            else:
                S.op("act", I("activation", out=pt[:].rearrange("p (a t) -> p a t", a=2)[:, :, 0:GT],
                              in_=bank(sb_, 2).rearrange("p (a t) -> p a t", a=2)[:, :, 0:GT], func=AF.Exp, scale=0.125),
                     reads=[PB[sb_], PB[sb_ + 1]], writes=[bpt])
            want = ((i + 1) * n_side + n_it - 1) // n_it
            while emitted < want and side:
                side.pop(0)(sb_)
                emitted += 1
            if i + 2 < n_it:
                emit_S(i + 2)
            S.group("pe", [
                I("matmul", oA[:, 0:GT], lhsT=V[:, kc, 0:128], rhs=pt[:, 0:GT], start=(kc == 0), stop=(kc == nkc - 1)),
                I("matmul", oB[:, 0:GT], lhsT=V[:, kc, 64:192], rhs=pt[:, 512:512 + GT], start=(kc == 0), stop=(kc == nkc - 1)),
            ], reads=[B_V[kc], bpt], writes=[PB[ob], PB[ob + 1]])
            if kc == nkc - 1:
                S.op("dve", I("reciprocal", out=rd[64:128, 0:GT], in_=oA[64:128, 0:GT]), reads=[PB[ob]], writes=[B_rd])
                S.op("dve", I("reciprocal", out=rd[0:64, 0:GT], in_=oB[0:64, 0:GT]), reads=[PB[ob + 1]], writes=[B_rd])
                S.op("dve", I("tensor_tensor", out=QT[0:64, hp, qs], in0=oA[0:64, 0:GT], in1=rd[64:128, 0:GT], op=ALU.mult),
                     reads=[PB[ob], B_rd], writes=[B_QT[hp][g]])
                S.op("dve", I("tensor_tensor", out=QT[64:128, hp, qs], in0=oB[64:128, 0:GT], in1=rd[0:64, 0:GT], op=ALU.mult),
                     reads=[PB[ob + 1], B_rd], writes=[B_QT[hp][g]])
        while side:
            side.pop(0)(0)

    wctr = {"n": 0}
    NWS = len(wpool)

    def stream_w(src_ap, src_buf):
        i = wctr["n"] % NWS
        wctr["n"] += 1
        S.dma("sp", I("dma_start", out=wpool[i][:], in_=src_ap), reads=[RG, src_buf], writes=[B_wpool[i]])
        return wpool[i], B_wpool[i]

    def mlp_phase(x_src, y_dst):
        for g in range(NGJ):
            def outproj(tt):
                t = g * TPG + tt
                ts_ = slice(t * 128, (t + 1) * 128)
                ob = 2 * tt
                fns = []
                for half in range(2):
                    for c in range(8):
                        lhs = QT[:, c, ts_] if c < 4 else CA[:, c - 4, ts_]
                        fns.append(I("matmul", bank(ob + half), lhsT=lhs, rhs=w_out_bf[:, c, half * 512:(half + 1) * 512],
                                     start=(c == 0), stop=(c == 7)))
                S.group("pe", fns, reads=[B_QT[c][g] for c in range(4)] + [B_CA[g], B_wout], writes=[PB[ob], PB[ob + 1]])
                sl = state["slot"]
                state["slot"] ^= 1
                S.dma("sp", I("dma_start", out=xt[sl][:], in_=x_src(t)), writes=[B_xt[sl]])
                S.op("dve", I("tensor_tensor", out=hbuf[:, tt, :], in0=bank(ob, 2), in1=xt[sl][:], op=ALU.add),
                     reads=[RG, PB[ob], PB[ob + 1], B_xt[sl]], writes=[B_h[tt]])

            def cast_T(tt):
                sl = tt % 2
                S.op("dve", I("tensor_copy", out=xnb[sl][:], in_=hbuf[:, tt, :]), reads=[RG, B_h[tt]], writes=[B_xnb[sl]])
                tb = 2 * tt
                pv = bank_bf(tb)
                S.group("pe", [I("transpose", out=pv[:, k * 128:(k + 1) * 128], in_=xnb[sl][:, k * 128:(k + 1) * 128],
                                 identity=ident[:]) for k in range(8)], reads=[B_xnb[sl], B_const], writes=[PB[tb]])
                S.op("act", I("copy", out=xnT[:, :, tt * 128:(tt + 1) * 128], in_=pv.rearrange("p (k t) -> p k t", k=8)),
                     reads=[PB[tb]], writes=[B_xnT])

            def stats(tt):
                junk = relu2.bitcast(BF16)
                S.op("act", I("activation", out=junk[:, 0:D], in_=hbuf[:, tt, :], func=AF.Square, accum_out=rs2[:, tt:tt + 1]),
                     reads=[RG, B_h[tt]], writes=[B_relu[0], B_relu[1], B_rs2[tt]])
                S.op("dve", I("tensor_scalar", out=rs2[:, tt:tt + 1], in0=rs2[:, tt:tt + 1], scalar1=1.0 / D, scalar2=EPS,
                              op0=ALU.mult, op1=ALU.add), reads=[B_rs2[tt]], writes=[B_rs2[tt]])
                S.op("dve", I("reciprocal", out=rs2[:, tt:tt + 1], in_=rs2[:, tt:tt + 1]), reads=[B_rs2[tt]], writes=[B_rs2[tt]])

            outproj(0)
            for tt in range(TPG):
                if tt + 1 < TPG:
                    outproj(tt + 1)
                cast_T(tt)
            for tt in range(TPG):
                stats(tt)
            for f in range(NFF):
                w_ap, w_buf = stream_w(wus[f], B_wus[f])
                wv = w_ap[:].rearrange("p (k c) -> p k c", k=8)
                ub = 4 + f % 4
                S.group("pe", [I("matmul", bank(ub)[:, 0:GT], lhsT=wv[:, k, :], rhs=xnT[:, k, :], start=(k == 0), stop=(k == 7))
                               for k in range(8)], reads=[RG, w_buf, B_xnT], writes=[PB[ub]])
                if f % 2 == 0:
                    S.op("act", I("activation", out=relu[0][:], in_=bank(ub)[:, 0:GT], func=AF.Relu), reads=[RG, PB[ub]], writes=[B_relu[0]])
                    S.op("act", I("activation", out=u2T[:, f, :], in_=relu[0][:], func=AF.Square), reads=[RG, B_relu[0]], writes=[B_u2T[f]])
                else:
                    S.op("dve", I("tensor_scalar", out=relu[1][:], in0=bank(ub)[:, 0:GT], scalar1=0.0, scalar2=None, op0=ALU.max),
                         reads=[RG, PB[ub]], writes=[B_relu[1]])
                    S.op("dve", I("tensor_tensor", out=u2T[:, f, :], in0=relu[1][:], in1=relu[1][:], op=ALU.mult),
                         reads=[RG, B_relu[1]], writes=[B_u2T[f]])
            for f in range(NFF):
                w_ap, w_buf = stream_w(wds[f], B_wds[f])
                fns = []
                for tt in range(TPG):
                    for half in range(2):
                        fns.append(I("matmul", bank(2 * tt + half), lhsT=u2T[:, f, tt * 128:(tt + 1) * 128],
                                     rhs=w_ap[:, half * 512:(half + 1) * 512], start=(f == 0), stop=(f == NFF - 1)))
                S.group("pe", fns, reads=[RG, B_u2T[f], w_buf], writes=[PB[i] for i in range(2 * TPG)])
            for tt in range(TPG):
                t = g * TPG + tt
                sl = tt % 2
                S.op("dve", I("scalar_tensor_tensor", out=hbuf[:, tt, :], in0=bank(2 * tt, 2), scalar=rs2[:, tt:tt + 1],
                              in1=hbuf[:, tt, :], op0=ALU.mult, op1=ALU.add),
                     reads=[RG, PB[2 * tt], PB[2 * tt + 1], B_h[tt], B_rs2[tt]], writes=[B_h[tt]])
                S.op("act", I("activation", out=xnb[sl][:], in_=hbuf[:, tt, :], func=AF.Square, accum_out=ssq[sl][:]),
                     reads=[RG, B_h[tt]], writes=[B_xnb[sl], B_st[sl]])
                S.op("act", I("activation", out=rstd[sl][:], in_=ssq[sl][:], func=AF.Sqrt, bias=EPS, scale=1.0 / D),
                     reads=[B_st[sl]], writes=[B_st[sl]])
                S.op("dve", I("reciprocal", out=rstd[sl][:], in_=rstd[sl][:]), reads=[B_st[sl]], writes=[B_st[sl]])
                S.op("dve", I("scalar_tensor_tensor", out=hbuf[:, tt, :], in0=hbuf[:, tt, :], scalar=rstd[sl][:, 0:1],
                              in1=gfin[:], op0=ALU.mult, op1=ALU.mult),
                     reads=[RG, B_h[tt], B_st[sl], B_const], writes=[B_h[tt]])
                S.dma("pool", I("dma_start", out=y_dst(t), in_=hbuf[:, tt, :]),
                      reads=[RG, B_h[tt]], writes=[B_out[octr["n"] % 8]])
                octr["n"] += 1

    def run_context(ctx_x, ctx_nt, jobs, pe_conv_ctx=False):
        if cfg.get("stop") == "pro":
            return
        first = bool(pro_thunks)
        c_items, c_st = ctx_stages(ctx_x, ctx_nt, lambda t: tab[t * 128:(t + 1) * 128, :], side=pro_thunks)
        if first or not jobs or cfg.get("stop") in ("ctx",):
            pipeline(c_items, c_st)
            c_items = []
            while pro_thunks:
                pro_thunks.pop(0)()
            region_switch()
        if cfg.get("stop") == "ctx":
            return
        for ji, (x_src, tab_src, y_dst, hl, hr) in enumerate(jobs):
            pe_conv_job = pe_conv_ctx and NGJ <= 2
            j_items, j_st = job_stages(x_src, tab_src, hl, hr, cbufb if pe_conv_job else cbuf)
            if ji == 0 and c_items:
                pipeline(c_items + j_items,
                         [(lambda i, it, a=a, b=b: (a if it[0] == "C" else b)(i, it)) for a, b in zip(c_st, j_st)])
            else:
                pipeline(j_items, j_st)
            if cfg.get("stop") == "p1":
                continue
            if pe_conv_job:
                conv_pe()
                side = conv_ln_thunks(do_conv=False)
            else:
                side = conv_ln_thunks()
            if cfg.get("stop") == "conv":
                while side:
                    side.pop(0)(0)
                continue
            attn_phase(ctx_nt, side)
            if cfg.get("stop") == "attn":
                continue
            region_switch()
            mlp_phase(x_src, y_dst)
            region_switch()

    if LP > 0:
        jobs = []
        for j in range(LQ // QJ):
            base = 128 + j * QJ
            jobs.append((
                (lambda t, base=base: xq[base + t * 128: base + (t + 1) * 128, :]),
                (lambda t, j=j: tabq[j * QJ + t * 128: j * QJ + (t + 1) * 128, :]),
                (lambda t, j=j: yp[j * QJ + t * 128: j * QJ + (t + 1) * 128, :]),
                True, True))
        run_context(lambda t: xp[t * 128:(t + 1) * 128, :], LP // 128, jobs)
    for s in range(NS):
        jobs = []
        nj = LS // QJ
        for j in range(nj):
            base = s * LS + j * QJ
            jobs.append((
                (lambda t, base=base: xs[base + t * 128: base + (t + 1) * 128, :]),
                (lambda t, j=j: tab[j * QJ + t * 128: j * QJ + (t + 1) * 128, :]),
                (lambda t, base=base: ys[base + t * 128: base + (t + 1) * 128, :]),
                j > 0, j < nj - 1))
        run_context(lambda t, s=s: xs[s * LS + t * 128: s * LS + (t + 1) * 128, :], LS // 128, jobs,
                    pe_conv_ctx=cfg.get("pe_conv", True))

    S.final_wait("pool", B_out)
    S.final_wait("sp", B_out)
    S.emit()
    st.close()
    return nc, S


def rope_table(npos):
    t = np.arange(npos)
    row = (t // 64).astype(np.float32)
    col = (t % 64).astype(np.float32)
    inv_freq = (np.float32(10000.0) ** (-(np.arange(0, 32, 2, dtype=np.float32)) / np.float32(32))).astype(np.float32)
    ar = row[:, None] * inv_freq[None, :]
    ac = col[:, None] * inv_freq[None, :]
    return np.concatenate([np.cos(ar), np.cos(ac), np.sin(ar), np.sin(ac)], axis=1).astype(np.float32)


def prep_shared(cfg, norm_mix_g, w_in, q_norm_g, k_norm_g, conv_dw_w, conv_dw_b, conv_ln_g, conv_ln_b,
                w_out, norm_mlp_g, w_up, w_down, norm_final_g):
    DFF = cfg["DFF"]
    NFF = DFF // 128
    qperm = np.concatenate([np.concatenate([np.arange(c * 64, (c + 1) * 64), np.arange((4 + c) * 64, (5 + c) * 64)])
                            for c in range(4)])
    colperm = np.concatenate([qperm, np.arange(512, INW)])
    w_in_p = np.asarray(w_in[0])[:, colperm]
    rowperm = np.concatenate([qperm, np.arange(512, 1024)])
    w_out_p = np.asarray(w_out[0])[rowperm, :]
    sh = {}
    sh["w_in_h"] = np.ascontiguousarray(w_in_p.reshape(8, 128, INW).transpose(1, 0, 2))
    sh["w_out_h"] = np.ascontiguousarray(w_out_p.reshape(8, 128, D).transpose(1, 0, 2))
    sh["w_up_h"] = np.ascontiguousarray(np.asarray(w_up[0]).reshape(8, 128, NFF, 128).transpose(2, 1, 0, 3))
    sh["w_down_h"] = np.ascontiguousarray(np.asarray(w_down[0]).reshape(NFF, 128, D))
    sh["gmix"] = np.ascontiguousarray(np.asarray(norm_mix_g[0]).reshape(8, 128).T)
    sh["gmlp"] = np.ascontiguousarray(np.asarray(norm_mlp_g[0]).reshape(8, 128).T)
    gq = np.tile(np.asarray(q_norm_g[0]), 8)
    gk = np.tile(np.asarray(k_norm_g[0]), 2)
    sh["gqk"] = np.ascontiguousarray(np.broadcast_to(np.concatenate([gq, gk])[None, :], (128, 640))).astype(np.float32)
    sh["gfin"] = np.ascontiguousarray(np.broadcast_to(np.asarray(norm_final_g)[None, :], (128, D))).astype(np.float32)
    sh["convw"] = np.ascontiguousarray(np.asarray(conv_dw_w[0]).reshape(CONV_K, 4, 128).transpose(2, 1, 0))
    sh["convb"] = np.ascontiguousarray(np.asarray(conv_dw_b[0]).reshape(4, 128).T)
    sh["lng"] = np.ascontiguousarray(np.asarray(conv_ln_g[0]).reshape(4, 128).T)
    sh["lnb"] = np.ascontiguousarray(np.asarray(conv_ln_b[0]).reshape(4, 128).T)
    sh["ident"] = np.eye(128, dtype=np.float32).astype(ml_dtypes.bfloat16)
    sh["onesd"] = np.full((128, 128), 1.0 / CW, dtype=np.float32).astype(ml_dtypes.bfloat16)
    return {k: (v if v.dtype == ml_dtypes.bfloat16 else np.ascontiguousarray(v, dtype=np.float32)) for k, v in sh.items()}


_PROGRAM_CACHE = {}


def kernel(x_prompt, x_sample, norm_mix_g, w_in, q_norm_g, k_norm_g, conv_dw_w, conv_dw_b, conv_ln_g, conv_ln_b,
           w_out, norm_mlp_g, w_up, w_down, norm_final_g):
    cfg = default_cfg()
    NS, LS, LP, LQ = cfg["NS"], cfg["LS"], cfg["LP"], cfg["LQ"]
    x_prompt = np.asarray(x_prompt, dtype=np.float32)
    x_sample = np.asarray(x_sample, dtype=np.float32)
    sh = prep_shared(cfg, norm_mix_g, w_in, q_norm_g, k_norm_g, conv_dw_w, conv_dw_b, conv_ln_g, conv_ln_b,
                     w_out, norm_mlp_g, w_up, w_down, norm_final_g)
    tab_full = rope_table(max(LP, LS))
    in_maps = []
    zeros_tile = np.zeros((128, D), np.float32)
    for c in range(N_CORES):
        b, half = c // 2, c % 2
        seq = x_prompt[b]
        own = seq[half * LQ:(half + 1) * LQ]
        left = seq[half * LQ - 128: half * LQ] if half == 1 else zeros_tile
        right = seq[(half + 1) * LQ:(half + 1) * LQ + 128] if half == 0 else zeros_tile
        m = dict(sh)
        m["xs"] = np.ascontiguousarray(x_sample[c * NS:(c + 1) * NS].reshape(NS * LS, D))
        m["xp"] = np.ascontiguousarray(seq)
        m["xq"] = np.ascontiguousarray(np.concatenate([left, own, right], axis=0))
        m["tab"] = tab_full
        m["tabq"] = np.ascontiguousarray(tab_full[half * LQ:(half + 1) * LQ])
        in_maps.append(m)
    if "nc" not in _PROGRAM_CACHE:
        _PROGRAM_CACHE["nc"] = build_program(cfg)[0]
    nc = _PROGRAM_CACHE["nc"]
    res = run_bass_kernel_spmd(nc, in_maps, core_ids=list(range(N_CORES)))
    y_prompt = np.empty((4, 2 * LQ, D), np.float32)
    y_sample = np.empty((N_CORES * NS, LS, D), np.float32)
    for c in range(N_CORES):
        r = res.results[c]
        b, half = c // 2, c % 2
        y_prompt[b, half * LQ:(half + 1) * LQ] = np.asarray(r["yp"]).reshape(LQ, D)
        y_sample[c * NS:(c + 1) * NS] = np.asarray(r["ys"]).reshape(NS, LS, D)
    return (y_prompt, y_sample)
```

```python
import contextlib
import numpy as np
import ml_dtypes
import concourse.bass as bass
import concourse.mybir as mybir
from concourse.bass_utils import run_bass_kernel_spmd

F32 = mybir.dt.float32
BF16 = mybir.dt.bfloat16
AF = mybir.ActivationFunctionType
ALU = mybir.AluOpType
AX = mybir.AxisListType

D = 1024
INW = 1792
QW = 512
CW = 512
HD = 64
CONV_K = 31
PAD = 15
EPS = 1e-6
LN_EPS = 1e-5
N_CORES = 8

ENGS = ("pe", "act", "dve", "pool", "sp")


def I(method, *args, **kw):
    return (method, args, kw)


class Buf:
    __slots__ = ("name", "w", "r")

    def __init__(self, name):
        self.name = name
        self.w = None
        self.r = []


class Sched:
    def __init__(self, nc, n_dma_sems=24):
        self.nc = nc
        self.ops = {e: [] for e in ENGS}
        self.sem_names = list(ENGS[:4]) + ["d%d" % i for i in range(n_dma_sems)]
        self.count = {s: 0 for s in self.sem_names}
        self.seen = {e: {s: 0 for s in self.sem_names} for e in ENGS}
        self.dma_rr = {}
        self.dma_pool = {"sp": (0, n_dma_sems * 2 // 3), "pool": (n_dma_sems * 2 // 3, n_dma_sems)}
        self.n_instr = {e: 0 for e in ENGS}
        self.n_wait = {e: 0 for e in ENGS}

    def _waits_for(self, eng, reads, writes):
        need = {}
        seen = self.seen[eng]

        def add(tok):
            s, v = tok
            if s == "pe" and eng == "pe":
                return
            if seen[s] < v and need.get(s, 0) < v:
                need[s] = v

        for b in reads:
            if b.w is not None:
                add(b.w)
        for b in writes:
            if b.w is not None:
                add(b.w)
            for t in b.r:
                add(t)
        for s, v in need.items():
            seen[s] = v
        return list(need.items())

    def _commit(self, tok, reads, writes):
        for b in reads:
            s = tok[0]
            b.r = [t for t in b.r if t[0] != s]
            b.r.append(tok)
        for b in writes:
            b.w = tok
            b.r = []

    def op(self, eng, fn, reads=(), writes=()):
        waits = self._waits_for(eng, reads, writes)
        self.count[eng] += 1
        tok = (eng, self.count[eng])
        self.ops[eng].append((waits, fn, (eng, 1)))
        self.n_instr[eng] += 1
        self.n_wait[eng] += len(waits)
        self._commit(tok, reads, writes)
        return tok

    def group(self, eng, fns, reads=(), writes=()):
        waits = self._waits_for(eng, reads, writes)
        self.count[eng] += 1
        tok = (eng, self.count[eng])
        n = len(fns)
        for i, fn in enumerate(fns):
            self.ops[eng].append((waits if i == 0 else [], fn, (eng, 1) if i == n - 1 else None))
        self.n_instr[eng] += n
        self.n_wait[eng] += len(waits)
        self._commit(tok, reads, writes)
        return tok

    def dma(self, q, fn, reads=(), writes=()):
        waits = self._waits_for(q, reads, writes)
        lo, hi = self.dma_pool[q]
        i = self.dma_rr.get(q, lo)
        sem = "d%d" % i
        self.dma_rr[q] = lo + ((i + 1 - lo) % (hi - lo))
        prev = self.count[sem]
        if prev > self.seen[q][sem]:
            self.seen[q][sem] = prev
            waits = [w for w in waits if w[0] != sem] + [(sem, prev)]
        self.count[sem] += 16
        tok = (sem, self.count[sem])
        self.ops[q].append((waits, fn, (sem, 16)))
        self.n_instr[q] += 1
        self.n_wait[q] += len(waits)
        self._commit(tok, reads, writes)
        return tok

    def final_wait(self, eng, bufs):
        waits = self._waits_for(eng, bufs, ())
        self.ops[eng].append((waits, None, None))

    def emit(self):
        nc = self.nc
        with contextlib.ExitStack() as st:
            sems = {s: st.enter_context(nc.semaphore("s_" + s)) for s in self.sem_names}
            block = st.enter_context(nc.Block())

            def run(e):
                def body(engh):
                    for waits, fn, inc in self.ops[e]:
                        for (s, v) in waits:
                            engh.wait_ge(sems[s], v)
                        if fn is None:
                            continue
                        ins = getattr(engh, fn[0])(*fn[1], **fn[2])
                        if inc is not None:
                            ins.then_inc(sems[inc[0]], inc[1])
                return body

            block.tensor(run("pe"))
            block.scalar(run("act"))
            block.vector(run("dve"))
            block.gpsimd(run("pool"))
            block.sync(run("sp"))


def default_cfg():
    return dict(NS=4, LS=2048, LP=8192, LQ=4096, QJ=1024, GT=512, DFF=4096)


def build_program(cfg):
    NS, LS, LP, LQ, QJ, GT, DFF = (cfg[k] for k in ("NS", "LS", "LP", "LQ", "QJ", "GT", "DFF"))
    NFF = DFF // 128
    NTJ = QJ // 128
    NGJ = QJ // GT
    TPG = GT // 128
    LMAX = max(LP, LS)
    NKC_MAX = LMAX // 128

    nc = bass.Bass("TRN2", target_bir_lowering=False)

    def din(name, shape, dt=F32):
        return nc.dram_tensor(name, list(shape), dt, kind="ExternalInput").ap()

    xs = din("xs", [NS * LS, D])
    xp = din("xp", [LP, D])
    xq = din("xq", [LQ + 256, D])
    tab = din("tab", [LMAX, 64])
    tabq = din("tabq", [LQ, 64])
    w_in_h = din("w_in_h", [128, 8, INW])
    w_out_h = din("w_out_h", [128, 8, D])
    w_up_h = din("w_up_h", [NFF, 128, 8, 128])
    w_down_h = din("w_down_h", [NFF, 128, D])
    gmix_h = din("gmix", [128, 8])
    gmlp_h = din("gmlp", [128, 8])
    gqk_h = din("gqk", [128, 640])
    gfin_h = din("gfin", [128, D])
    convw_h = din("convw", [128, 4, CONV_K])
    convb_h = din("convb", [128, 4])
    lng_h = din("lng", [128, 4])
    lnb_h = din("lnb", [128, 4])
    ident_h = din("ident", [128, 128], BF16)
    onesd_h = din("onesd", [128, 128], BF16)
    ys = nc.dram_tensor("ys", [NS * LS, D], F32, kind="ExternalOutput").ap()
    yp = nc.dram_tensor("yp", [LQ, D], F32, kind="ExternalOutput").ap()
    wus = nc.dram_tensor("wus", [NFF, 128, 1024], BF16, kind="Internal").ap()
    wds = nc.dram_tensor("wds", [NFF, 128, 1024], BF16, kind="Internal").ap()

    S = Sched(nc)
    st = contextlib.ExitStack()

    def sb(name, shape, dt):
        return st.enter_context(nc.sbuf_tensor(name, list(shape), dt))

    w_in_bf = sb("w_in_bf", [128, 8, INW], BF16)
    w_out_bf = sb("w_out_bf", [128, 8, D], BF16)
    ident = sb("ident_sb", [128, 128], BF16)
    onesd = sb("onesd_sb", [128, 128], BF16)
    gqk = sb("gqk_sb", [128, 640], F32)
    gfin = sb("gfin_sb", [128, D], F32)
    gmix = sb("gmix_sb", [128, 8], F32)
    gmlp = sb("gmlp_sb", [128, 8], F32)
    convw = sb("convw_sb", [128, 4, CONV_K], F32)
    convb = sb("convb_sb", [128, 4], F32)
    lng = sb("lng_sb", [128, 4], F32)
    lnb = sb("lnb_sb", [128, 4], F32)
    KT = sb("KT", [128, LMAX], BF16)
    V = sb("V", [128, NKC_MAX, 192], BF16)
    QT = sb("QT", [128, 4, QJ], BF16)
    CA = sb("CA", [128, 4, QJ], BF16)
    NPT = 3
    PT = [sb("PT%d" % i, [128, 1024], BF16) for i in range(NPT)]
    xt = [sb("xt%d" % i, [128, D], F32) for i in range(2)]
    tabt = [sb("tabt%d" % i, [128, 64], F32) for i in range(5)]
    xnb = [sb("xnb%d" % i, [128, D], BF16) for i in range(2)]
    xnT = sb("xnT", [128, 8, GT], BF16)
    ssq = [sb("ssq%d" % i, [128, 1], F32) for i in range(2)]
    rstd = [sb("rstd%d" % i, [128, 1], F32) for i in range(2)]
    ssq2 = [sb("ssq2_%d" % i, [128, 8], F32) for i in range(2)]
    rstd2 = [sb("rstd2_%d" % i, [128, 8], F32) for i in range(2)]
    rd = sb("rd", [128, 512], F32)
    dg = [sb("dg%d" % i, [128, 128], BF16) for i in range(4)]
    scr1 = sb("scr1", [128, 8], F32)
    rs2 = sb("rs2", [128, 8], F32)

    P1_BYTES = (640 * 4) * 4 + 320 * 4 * 2 + 640 * 2 * 2 + 4 * (QJ + 2 * PAD) * 4 + 4 * QJ * 4 + 4 * GT * 2 + GT * 4 * 3 + 2 * 2048 + 8 * GT * 2
    P3_BYTES = NFF * GT * 2 + 8 * 2048 + TPG * D * 4 + GT * 4 * 2
    PRO_BYTES = P1_BYTES + 2 * 2048 * 4 + 2 * 2048 * 2 if (4 * (QJ + 2 * PAD) * 4 + 4 * QJ * 4) < 24576 else 0
    R_BYTES = max(P1_BYTES, P3_BYTES, PRO_BYTES)
    R = sb("R", [128, R_BYTES // 4 + 8], F32)

    class Carver:
        def __init__(self):
            self.off = 0

        def take(self, nelem, dt):
            nbytes = nelem * (4 if dt == F32 else 2)
            nw = (nbytes + 3) // 4
            a = R[:, self.off:self.off + nw]
            self.off += nw
            assert self.off * 4 <= R_BYTES + 32, (self.off * 4, R_BYTES)
            return a if dt == F32 else a.bitcast(BF16)

    c1 = Carver()
    sq = [c1.take(640, F32) for _ in range(2)]
    zn = [c1.take(640, F32) for _ in range(2)]
    t1 = c1.take(320, F32)
    t2 = c1.take(320, F32)
    qkb = [c1.take(640, BF16) for _ in range(2)]
    cbuf_off = c1.off
    cbuf_flat = c1.take(4 * (QJ + 2 * PAD), F32)
    cbuf = cbuf_flat.rearrange("p (c t) -> p c t", c=4)
    cbufb = cbuf_flat.bitcast(BF16)[:, 0:4 * (QJ + 2 * PAD)].rearrange("p (c t) -> p c t", c=4)
    co = c1.take(4 * QJ, F32).rearrange("p (c t) -> p c t", c=4)
    cobf = c1.take(4 * GT, BF16).rearrange("p (c t) -> p c t", c=4)
    sig2 = c1.take(2 * GT, F32)
    sig = [sig2[:, 0:GT], sig2[:, GT:2 * GT]]
    assert QJ <= 2 * GT
    ctmp = sig2[:, 0:QJ]
    lnr = c1.take(GT, F32)
    xnTh = [c1.take(8 * 128, BF16).rearrange("p (k t) -> p k t", k=8) for _ in range(2)]
    xnT1 = c1.take(8 * GT, BF16).rearrange("p (k t) -> p k t", k=8)
    xnTg = [xnT, xnT1]

    c3 = Carver()
    u2T = c3.take(NFF * GT, BF16).rearrange("p (f t) -> p f t", f=NFF)
    wpool = [c3.take(1024, BF16) for _ in range(8)]
    hbuf = c3.take(TPG * D, F32).rearrange("p (t d) -> p t d", t=TPG)
    relu2 = c3.take(2 * GT, F32)
    relu = [relu2[:, 0:GT], relu2[:, GT:2 * GT]]

    c0 = Carver()
    c0.off = cbuf_off if (4 * (QJ + 2 * PAD) * 4 + 4 * QJ * 4) >= 24576 else c1.off
    stg = [c0.take(2048, F32) for _ in range(2)]
    stgb = [c0.take(2048, BF16) for _ in range(2)]

    ps = st.enter_context(nc.psum_tensor("ps", [128, 4096], F32))

    def bank(b, n=1):
        return ps[:, b * 512:(b + n) * 512]

    def bank_bf(b):
        return ps[:, b * 512:(b + 1) * 512].bitcast(BF16)

    PB = [Buf("pb%d" % i) for i in range(8)]
    RG = Buf("RG")
    B_win, B_wout, B_const = Buf("win"), Buf("wout"), Buf("const")
    B_KT = [Buf("KT%d" % i) for i in range(NKC_MAX)]
    B_V = [Buf("V%d" % i) for i in range(NKC_MAX)]
    B_QT = [[Buf("QT%d_%d" % (c, g)) for g in range(NGJ)] for c in range(4)]
    B_CA = [Buf("CA%d" % g) for g in range(NGJ)]
    B_PT = [Buf("PT%d" % i) for i in range(NPT)]
    B_xt = [Buf("xt%d" % i) for i in range(2)]
    B_tabt = [Buf("tabt%d" % i) for i in range(5)]
    B_xnb = [Buf("xnb%d" % i) for i in range(2)]
    B_xnT = Buf("xnT")
    B_xnTh = [Buf("xnTh%d" % i) for i in range(2)]
    B_xnTg = [B_xnT, Buf("xnT1")]
    B_ctmp = Buf("ctmp")
    B_st = [Buf("st%d" % i) for i in range(2)]
    B_st2 = [Buf("st2_%d" % i) for i in range(2)]
    B_rd = Buf("rd")
    B_dg = [Buf("dg%d" % i) for i in range(4)]
    B_sq, B_zn, B_qkb = [Buf("sq0"), Buf("sq1")], [Buf("zn0"), Buf("zn1")], [Buf("qkb0"), Buf("qkb1")]
    B_t1, B_t2 = Buf("t1"), Buf("t2")
    B_cbuf = [Buf("cbuf%d" % c) for c in range(4)]
    B_co = [Buf("co%d" % c) for c in range(4)]
    B_cog = [Buf("cog%d" % g) for g in range(NGJ)]
    B_cobf, B_sig, B_lnr = Buf("cobf"), [Buf("sig0"), Buf("sig1")], Buf("lnr")
    B_u2T = [Buf("u2T%d" % f) for f in range(NFF)]
    B_wpool = [Buf("wp%d" % i) for i in range(8)]
    B_h = [Buf("h%d" % i) for i in range(TPG)]
    B_relu = [Buf("relu0"), Buf("relu1")]
    B_stg = [Buf("stg0"), Buf("stg1")]
    B_stgb = [Buf("stgb0"), Buf("stgb1")]
    B_wus = [Buf("wus%d" % f) for f in range(NFF)]
    B_wds = [Buf("wds%d" % f) for f in range(NFF)]
    B_out = [Buf("out%d" % i) for i in range(8)]
    octr = {"n": 0}
    B_scr = Buf("scr1")
    B_rs2 = [Buf("rs2_%d" % i) for i in range(8)]

    def region_switch():
        S.op("pool", I("memset", scr1[:], 0.0), reads=[], writes=[RG, B_scr])

    for dst, src in ((ident, ident_h), (onesd, onesd_h), (gqk, gqk_h), (gfin, gfin_h), (gmix, gmix_h),
                     (gmlp, gmlp_h), (convw, convw_h), (convb, convb_h), (lng, lng_h), (lnb, lnb_h)):
        S.dma("sp", I("dma_start", out=dst[:], in_=src), writes=[B_const])
    S.op("pool", I("memset", V[:, :, 64:128], 1.0), writes=B_V)

    sctr = {"n": 0}

    def nxt():
        j = sctr["n"] % 2
        sctr["n"] += 1
        return j

    for k in range(8):
        j = nxt()
        S.dma("sp", I("dma_start", out=stg[j][:, 0:INW], in_=w_in_h[:, k, :]), reads=[RG], writes=[B_stg[j]])
        S.op("dve", I("tensor_scalar", out=w_in_bf[:, k, :], in0=stg[j][:, 0:INW], scalar1=gmix[:, k:k + 1], scalar2=None,
                      op0=ALU.mult), reads=[RG, B_stg[j], B_const], writes=[B_win])
    for k in range(0, 8, 2):
        j = nxt()
        S.dma("sp", I("dma_start", out=stg[j][:, 0:2048].rearrange("p (k c) -> p k c", k=2), in_=w_out_h[:, k:k + 2, :]),
              reads=[RG], writes=[B_stg[j]])
        S.op("act", I("copy", out=w_out_bf[:, k:k + 2, :].rearrange("p k c -> p (k c)"), in_=stg[j][:, 0:2048]),
             reads=[RG, B_stg[j]], writes=[B_wout])

    FB = 2
    pro_thunks = []
    pend = {"out": None}

    def flush_out():
        if pend["out"] is not None:
            pend["out"]()
            pend["out"] = None

    for f0 in range(0, NFF, FB):
        def cv_up(f0=f0):
            j = nxt()
            S.dma("sp", I("dma_start", out=stg[j][:, 0:FB * 1024].rearrange("p (f n) -> p f n", f=FB),
                          in_=w_up_h[f0:f0 + FB].rearrange("f p k c -> p f (k c)")), reads=[RG], writes=[B_stg[j]])
            S.op("dve", I("tensor_tensor", out=stgb[j][:, 0:FB * 1024].rearrange("p (f k c) -> p f k c", f=FB, k=8),
                          in0=stg[j][:, 0:FB * 1024].rearrange("p (f k c) -> p f k c", f=FB, k=8),
                          in1=gmlp[:].unsqueeze(1).unsqueeze(3).to_broadcast([128, FB, 8, 128]), op=ALU.mult),
                 reads=[RG, B_stg[j], B_const], writes=[B_stgb[j]])
            flush_out()
            pend["out"] = lambda: S.dma("sp", I("dma_start", out=wus[f0:f0 + FB].rearrange("f p n -> p f n"),
                                                in_=stgb[j][:, 0:FB * 1024].rearrange("p (f n) -> p f n", f=FB)),
                                        reads=[RG, B_stgb[j]], writes=B_wus[f0:f0 + FB])
        pro_thunks.append(cv_up)
    for f0 in range(0, NFF, FB):
        def cv_dn(f0=f0):
            j = nxt()
            S.dma("sp", I("dma_start", out=stg[j][:, 0:FB * 1024].rearrange("p (f n) -> p f n", f=FB),
                          in_=w_down_h[f0:f0 + FB].rearrange("f p n -> p f n")), reads=[RG], writes=[B_stg[j]])
            S.op("act", I("copy", out=stgb[j][:, 0:FB * 1024], in_=stg[j][:, 0:FB * 1024]), reads=[RG, B_stg[j]], writes=[B_stgb[j]])
            flush_out()
            pend["out"] = lambda: S.dma("sp", I("dma_start", out=wds[f0:f0 + FB].rearrange("f p n -> p f n"),
                                                in_=stgb[j][:, 0:FB * 1024].rearrange("p (f n) -> p f n", f=FB)),
                                        reads=[RG, B_stgb[j]], writes=B_wds[f0:f0 + FB])
        pro_thunks.append(cv_dn)
    pro_thunks.append(flush_out)

    state = {"slot": 0}

    def pipeline(items, stages):
        ns = len(stages)
        for step in range(len(items) + ns - 1):
            for s_ in range(ns):
                i = step - s_
                if 0 <= i < len(items):
                    stages[s_](i, items[i])

    NTAB = len(tabt)

    def fe_a(i, x_ap, tab_ap):
        sl = i % 2
        S.dma("sp", I("dma_start", out=xt[sl][:], in_=x_ap), writes=[B_xt[sl]])
        if tab_ap is not None:
            S.dma("sp", I("dma_start", out=tabt[i % NTAB][:], in_=tab_ap), writes=[B_tabt[i % NTAB]])
        S.op("act", I("activation", out=xnb[sl][:], in_=xt[sl][:], func=AF.Square, accum_out=ssq[sl][:]),
             reads=[B_xt[sl]], writes=[B_xnb[sl], B_st[sl]])
        S.op("act", I("activation", out=rstd[sl][:], in_=ssq[sl][:], func=AF.Sqrt, bias=EPS, scale=1.0 / D),
             reads=[B_st[sl]], writes=[B_st[sl]])
        S.op("dve", I("reciprocal", out=rstd[sl][:], in_=rstd[sl][:]), reads=[B_st[sl]], writes=[B_st[sl]])
        S.op("dve", I("tensor_scalar", out=xnb[sl][:], in0=xt[sl][:], scalar1=rstd[sl][:, 0:1], scalar2=None, op0=ALU.mult),
             reads=[B_xt[sl], B_st[sl]], writes=[B_xnb[sl]])

    def fe_b(i, dstT, dstT_buf, cols):
        sl = i % 2
        tb = 2 + sl
        pv = bank_bf(tb)
        S.group("pe", [I("transpose", out=pv[:, k * 128:(k + 1) * 128], in_=xnb[sl][:, k * 128:(k + 1) * 128],
                         identity=ident[:]) for k in range(8)], reads=[B_xnb[sl], B_const], writes=[PB[tb]])
        S.op("dve", I("tensor_copy", out=dstT[:, :, cols], in_=pv.rearrange("p (k t) -> p k t", k=8)),
             reads=[RG, PB[tb]], writes=[dstT_buf])

    def qk_sq(i, zps, zbuf, nh):
        sl = i % 2
        S.op("act", I("activation", out=sq[sl][:, 0:nh * 64], in_=zps, func=AF.Square), reads=[RG, zbuf], writes=[B_sq[sl]])

    def qk_norm(i, zps, zbuf, nh, goff):
        sl = i % 2
        w = nh * 64
        S.op("dve", I("tensor_reduce", out=ssq2[sl][:, 0:nh], in_=sq[sl][:, 0:w].rearrange("p (h d) -> p h d", h=nh),
                      axis=AX.X, op=ALU.add), reads=[RG, B_sq[sl]], writes=[B_st2[sl]])
        S.op("act", I("activation", out=rstd2[sl][:, 0:nh], in_=ssq2[sl][:, 0:nh], func=AF.Sqrt, bias=EPS, scale=1.0 / HD),
             reads=[B_st2[sl]], writes=[B_st2[sl]])
        S.op("dve", I("reciprocal", out=rstd2[sl][:, 0:nh], in_=rstd2[sl][:, 0:nh]), reads=[B_st2[sl]], writes=[B_st2[sl]])
        S.op("dve", I("tensor_tensor", out=zn[sl][:, 0:w].rearrange("p (h d) -> p h d", h=nh),
                      in0=zps.rearrange("p (h d) -> p h d", h=nh),
                      in1=rstd2[sl][:, 0:nh].unsqueeze(2).to_broadcast([128, nh, HD]), op=ALU.mult),
             reads=[RG, zbuf, B_st2[sl]], writes=[B_zn[sl]])
        S.op("dve", I("tensor_tensor", out=zn[sl][:, 0:w], in0=zn[sl][:, 0:w], in1=gqk[:, goff:goff + w], op=ALU.mult),
             reads=[RG, B_zn[sl], B_const], writes=[B_zn[sl]])

    def rope(i, nh, re="pool"):
        sl = i % 2
        w = nh * 64
        tb_ = tabt[i % NTAB]
        btab = B_tabt[i % NTAB]
        zv = zn[sl][:, 0:w].rearrange("p (h a f i) -> p h a f i", h=nh, a=2, f=2)
        ov = qkb[sl][:, 0:w].rearrange("p (h a f i) -> p h a f i", h=nh, a=2, f=2)
        x1, x2 = zv[:, :, :, 0, :], zv[:, :, :, 1, :]
        o1, o2 = ov[:, :, :, 0, :], ov[:, :, :, 1, :]
        cosb = tb_[:, 0:32].rearrange("p (a i) -> p a i", a=2).unsqueeze(1).to_broadcast([128, nh, 2, 16])
        sinb = tb_[:, 32:64].rearrange("p (a i) -> p a i", a=2).unsqueeze(1).to_broadcast([128, nh, 2, 16])
        hw = nh * 32
        t1v = t1[:, 0:hw].rearrange("p (h a i) -> p h a i", h=nh, a=2)
        t2v = t2[:, 0:hw].rearrange("p (h a i) -> p h a i", h=nh, a=2)
        S.op(re, I("tensor_tensor", out=t1v, in0=x1, in1=cosb, op=ALU.mult), reads=[RG, B_zn[sl], btab], writes=[B_t1])
        S.op(re, I("tensor_tensor", out=t2v, in0=x2, in1=sinb, op=ALU.mult), reads=[RG, B_zn[sl], btab], writes=[B_t2])
        S.op(re, I("tensor_tensor", out=o1, in0=t1v, in1=t2v, op=ALU.subtract), reads=[RG, B_t1, B_t2], writes=[B_qkb[sl]])
        S.op(re, I("tensor_tensor", out=t1v, in0=x2, in1=cosb, op=ALU.mult), reads=[RG, B_zn[sl], btab], writes=[B_t1])
        S.op(re, I("tensor_tensor", out=t2v, in0=x1, in1=sinb, op=ALU.mult), reads=[RG, B_zn[sl], btab], writes=[B_t2])
        S.op(re, I("tensor_tensor", out=o2, in0=t1v, in1=t2v, op=ALU.add), reads=[RG, B_t1, B_t2], writes=[B_qkb[sl]])

    def ctx_stages(x_src, nt, tab_src, side=None):
        def stA1(i, it):
            t = it[1]
            fe_a(i, x_src(t), tab_src(t))
            if side:
                side.pop(0)()

        def stA2(i, it):
            fe_b(i, xnTh[i % 2], B_xnTh[i % 2], slice(0, 128))

        def stB1a(i, it):
            t = it[1]
            kb = i % 2
            S.group("pe", [I("matmul", bank(kb)[:, 0:256], lhsT=xnTh[i % 2][:, k, :], rhs=w_in_bf[:, k, 512:768],
                             start=(k == 0), stop=(k == 7)) for k in range(8)],
                    reads=[RG, B_xnTh[i % 2], B_win], writes=[PB[kb]])
            S.op("act", I("copy", out=V[:, t, :].rearrange("p (a d) -> p a d", a=3)[:, 0:3:2, :],
                          in_=bank(kb)[:, 128:256].rearrange("p (a d) -> p a d", a=2)), reads=[PB[kb]], writes=[B_V[t]])
            qk_sq(i, bank(kb)[:, 0:128], PB[kb], 2)

        def stB1b(i, it):
            kb = i % 2
            qk_norm(i, bank(kb)[:, 0:128], PB[kb], 2, 512)

        def stB2a(i, it):
            rope(i, 2)

        def stB2b(i, it):
            t = it[1]
            tb = 4 + i % 2
            pv = bank_bf(tb)
            S.op("pe", I("transpose", out=pv[:, 0:128], in_=qkb[i % 2][:, 0:128], identity=ident[:]),
                 reads=[RG, B_qkb[i % 2], B_const], writes=[PB[tb]])
            S.op("dve", I("tensor_copy", out=KT[:, t * 128:(t + 1) * 128], in_=pv[:, 0:128]), reads=[PB[tb]], writes=[B_KT[t]])

        return [("C", t) for t in range(nt)], [stA1, stA2, stB1a, stB1b, stB2a, stB2b]

    def glu_chunks(rhsT, rbuf, n, dst_col0, cdst):
        for c in range(4):
            bv, bg = (5 if c % 2 == 0 else 7), 6
            S.group("pe", [I("matmul", bank(bv)[:, 0:n], lhsT=w_in_bf[:, k, 768 + c * 128:768 + (c + 1) * 128],
                             rhs=rhsT[:, k, :], start=(k == 0), stop=(k == 7)) for k in range(8)],
                    reads=[RG, rbuf, B_win], writes=[PB[bv]])
            S.group("pe", [I("matmul", bank(bg)[:, 0:n], lhsT=w_in_bf[:, k, 1280 + c * 128:1280 + (c + 1) * 128],
                             rhs=rhsT[:, k, :], start=(k == 0), stop=(k == 7)) for k in range(8)],
                    reads=[RG, rbuf, B_win], writes=[PB[bg]])
            S.op("act", I("activation", out=sig[c % 2][:, 0:n], in_=bank(bg)[:, 0:n], func=AF.Sigmoid),
                 reads=[RG, PB[bg]], writes=[B_sig[c % 2]])
            S.op("dve", I("tensor_tensor", out=cdst[:, c, dst_col0:dst_col0 + n], in0=bank(bv)[:, 0:n],
                          in1=sig[c % 2][:, 0:n], op=ALU.mult), reads=[RG, PB[bv], B_sig[c % 2]], writes=[B_cbuf[c]])

    def job_stages(x_src, tab_src, has_left, has_right, cdst):
        items = []
        if has_left:
            items.append(("L", -1))
        else:
            S.op("pool", I("memset", cdst[:, :, 0:PAD], 0.0), reads=[RG], writes=B_cbuf)
        items += [("B", t) for t in range(NTJ)]
        if has_right:
            items.append(("R", NTJ))
        else:
            S.op("pool", I("memset", cdst[:, :, PAD + QJ:PAD + QJ + PAD], 0.0), reads=[RG], writes=B_cbuf)

        def stA1(i, it):
            kind, t = it
            fe_a(i, x_src(t), tab_src(t) if kind == "B" else None)

        def stA2(i, it):
            kind, t = it
            if kind == "B":
                g, tt = t // TPG, t % TPG
                fe_b(i, xnTg[g % 2], B_xnTg[g % 2], slice(tt * 128, (tt + 1) * 128))
            else:
                fe_b(i, xnTh[i % 2], B_xnTh[i % 2], slice(0, 128))

        def stB1a(i, it):
            kind, t = it
            if kind == "L":
                glu_chunks(xnTh[i % 2][:, :, 128 - PAD:128], B_xnTh[i % 2], PAD, 0, cdst)
            elif kind == "R":
                glu_chunks(xnTh[i % 2][:, :, 0:PAD], B_xnTh[i % 2], PAD, PAD + QJ, cdst)
            else:
                g, tt = t // TPG, t % TPG
                qb = i % 2
                S.group("pe", [I("matmul", bank(qb), lhsT=xnTg[g % 2][:, k, tt * 128:(tt + 1) * 128], rhs=w_in_bf[:, k, 0:512],
                                 start=(k == 0), stop=(k == 7)) for k in range(8)],
                        reads=[RG, B_xnTg[g % 2], B_win], writes=[PB[qb]])
                qk_sq(i, bank(qb), PB[qb], 8)
                if tt == TPG - 1:
                    glu_chunks(xnTg[g % 2][:, :, :], B_xnTg[g % 2], GT, PAD + g * GT, cdst)

        def stB1b(i, it):
            kind, t = it
            if kind == "B":
                qk_norm(i, bank(i % 2), PB[i % 2], 8, 0)

        def stB2a(i, it):
            if it[0] == "B":
                rope(i, 8)

        def stB2b(i, it):
            kind, t = it
            if kind != "B":
                return
            g = t // TPG
            pv = bank_bf(4)
            S.group("pe", [I("transpose", out=pv[:, c * 128:(c + 1) * 128], in_=qkb[i % 2][:, c * 128:(c + 1) * 128],
                             identity=ident[:]) for c in range(4)], reads=[RG, B_qkb[i % 2], B_const], writes=[PB[4]])
            S.op("act", I("copy", out=QT[:, :, t * 128:(t + 1) * 128], in_=pv[:, 0:512].rearrange("p (c t) -> p c t", c=4)),
                 reads=[PB[4]], writes=[B_QT[c][g] for c in range(4)])

        return items, [stA1, stA2, stB1a, stB1b, stB2a, stB2b]

    def conv_ln_thunks(do_conv=True, do_ln=True):
        th = []
        POOL_CHUNKS = cfg.get("pool_chunks", ())
        per_chunk = []
        for c in (range(4) if do_conv else []):
            lst = []
            if c in POOL_CHUNKS:
                lst.append(lambda fb, c=c: S.op("pool", I("tensor_scalar", out=co[:, c, :], in0=cbuf[:, c, 0:QJ],
                                                          scalar1=convw[:, c, 0:1], scalar2=convb[:, c:c + 1], op0=ALU.mult, op1=ALU.add),
                                                reads=[RG, B_cbuf[c], B_const], writes=[B_co[c]]))
                for j in range(1, CONV_K):
                    def tap(fb, c=c, j=j):
                        S.op("pool", I("tensor_scalar", out=ctmp[:], in0=cbuf[:, c, j:j + QJ], scalar1=convw[:, c, j:j + 1],
                                       scalar2=None, op0=ALU.mult), reads=[RG, B_cbuf[c], B_const], writes=B_sig)
                        S.op("pool", I("tensor_tensor", out=co[:, c, :], in0=co[:, c, :], in1=ctmp[:], op=ALU.add),
                             reads=[RG, B_co[c]] + B_sig, writes=[B_co[c]])
                    lst.append(tap)
            else:
                lst.append(lambda fb, c=c: S.op("dve", I("tensor_scalar", out=co[:, c, :], in0=cbuf[:, c, 0:QJ],
                                                         scalar1=convw[:, c, 0:1], scalar2=convb[:, c:c + 1], op0=ALU.mult, op1=ALU.add),
                                                reads=[RG, B_cbuf[c], B_const], writes=[B_co[c]]))
                for j in range(1, CONV_K):
                    lst.append(lambda fb, c=c, j=j: S.op("dve", I("scalar_tensor_tensor", out=co[:, c, :], in0=cbuf[:, c, j:j + QJ],
                                                                 scalar=convw[:, c, j:j + 1], in1=co[:, c, :], op0=ALU.mult, op1=ALU.add),
                                                         reads=[RG, B_cbuf[c], B_const, B_co[c]], writes=[B_co[c]]))
            per_chunk.append(lst)
        dve_l = [x for c in range(len(per_chunk)) if c not in POOL_CHUNKS for x in per_chunk[c]]
        pool_l = [x for c in range(len(per_chunk)) if c in POOL_CHUNKS for x in per_chunk[c]]
        n_d, n_p = len(dve_l), len(pool_l)
        pi = 0
        for di, x in enumerate(dve_l):
            th.append(x)
            while pi < n_p and (pi + 1) * n_d <= (di + 1) * n_p:
                th.append(pool_l[pi])
                pi += 1
        th += pool_l[pi:]
        th_g = []
        for g in range(NGJ if (cfg.get("noln") is None and do_ln) else 0):
            gs = slice(g * GT, (g + 1) * GT)
            if g % 2 == 0:
                cb, lr, Bcb, Blr = cobf, lnr, B_cobf, B_lnr
            else:
                flat = xnT1.rearrange("p k t -> p (k t)")
                cb = flat[:, 0:4 * GT].rearrange("p (c t) -> p c t", c=4)
                lr = flat[:, 4 * GT:6 * GT].bitcast(F32)
                Bcb = Blr = B_xnTg[1]
            Bg = B_cog[g]
            lst = []
            lst.append(lambda fb, gs=gs, cb=cb, Bcb=Bcb, Bg=Bg: S.op(
                "dve", I("tensor_copy", out=cb[:], in_=co[:, :, gs]), reads=[RG, Bg] + B_co, writes=[Bcb]))

            def mean_sub(fb, gs=gs, cb=cb, Bcb=Bcb, Bg=Bg, g=g):
                bk = fb + (g % 2)
                S.group("pe", [I("matmul", bank(bk)[:, 0:GT], lhsT=onesd[:], rhs=cb[:, c, :],
                               start=(c == 0), stop=(c == 3)) for c in range(4)],
                        reads=[RG, Bcb, B_const], writes=[PB[bk]])
                S.op("dve", I("tensor_tensor", out=co[:, :, gs], in0=co[:, :, gs],
                              in1=bank(bk)[:, 0:GT].unsqueeze(1).to_broadcast([128, 4, GT]), op=ALU.subtract),
                     reads=[RG, PB[bk], Bg], writes=[Bg])
            lst.append(mean_sub)
            lst.append(lambda fb, gs=gs, cb=cb, Bcb=Bcb, Bg=Bg: S.op(
                "act", I("activation", out=cb[:], in_=co[:, :, gs], func=AF.Square), reads=[RG, Bg], writes=[Bcb]))

            def var_sqrt(fb, cb=cb, lr=lr, Bcb=Bcb, Blr=Blr, g=g):
                bk = fb + (g % 2)
                S.group("pe", [I("matmul", bank(bk)[:, 0:GT], lhsT=onesd[:], rhs=cb[:, c, :],
                               start=(c == 0), stop=(c == 3)) for c in range(4)],
                        reads=[RG, Bcb, B_const], writes=[PB[bk]])
                S.op("act", I("activation", out=lr[:], in_=bank(bk)[:, 0:GT], func=AF.Sqrt, bias=LN_EPS, scale=1.0),
                     reads=[RG, PB[bk]], writes=[Blr])
            lst.append(var_sqrt)
            lst.append(lambda fb, lr=lr, Blr=Blr: S.op("dve", I("reciprocal", out=lr[:], in_=lr[:]), reads=[RG, Blr], writes=[Blr]))
            lst.append(lambda fb, gs=gs, lr=lr, Blr=Blr, Bg=Bg: S.op(
                "dve", I("tensor_tensor", out=co[:, :, gs], in0=co[:, :, gs], in1=lr[:].unsqueeze(1).to_broadcast([128, 4, GT]),
                         op=ALU.mult), reads=[RG, Blr, Bg], writes=[Bg]))

            def silu4(fb, gs=gs, g=g, Bg=Bg):
                for c in range(4):
                    S.op("act", I("activation", out=CA[:, c, gs], in_=co[:, c, gs], func=AF.Silu, scale=lng[:, c:c + 1],
                                  bias=lnb[:, c:c + 1]), reads=[RG, Bg, B_const], writes=[B_CA[g]])
            lst.append(silu4)
            th_g.append(lst)
        for g0 in range(0, len(th_g), 2):
            pair = th_g[g0:g0 + 2]
            for k in range(max(len(p) for p in pair)):
                for p in pair:
                    if k < len(p):
                        th.append(p[k])
        if cfg.get('lnsteps') is not None:
            th = th[:4 * CONV_K + cfg['lnsteps']]
        return th

    def conv_pe():
        dctr = 0
        for c in range(4):
            b0 = 2 * (c % 2)
            for j in range(CONV_K):
                sl = dctr % 4
                dctr += 1
                S.op("act", I("activation", out=dg[sl][:], in_=ident[:], func=AF.Identity, scale=convw[:, c, j:j + 1]),
                     reads=[B_const], writes=[B_dg[sl]])
                S.group("pe", [I("matmul", bank(b0 + g)[:, 0:GT], lhsT=dg[sl][:], rhs=cbufb[:, c, g * GT + j:g * GT + j + GT],
                                 start=(j == 0), stop=(j == CONV_K - 1)) for g in range(NGJ)],
                        reads=[RG, B_dg[sl], B_cbuf[c]], writes=[PB[b0 + g] for g in range(NGJ)])
            for g in range(NGJ):
                S.op("dve", I("tensor_scalar", out=co[:, c, g * GT:(g + 1) * GT], in0=bank(b0 + g)[:, 0:GT],
                              scalar1=convb[:, c:c + 1], scalar2=None, op0=ALU.add),
                     reads=[RG, PB[b0 + g], B_const], writes=[B_co[c]])

    def attn_phase(nkc, side):
        its = [(hp, g, kc) for hp in range(4) for g in range(NGJ) for kc in range(nkc)]
        n_it = len(its)
        n_side = len(side)
        emitted = 0

        def emit_S(i):
            hp, g, kc = its[i]
            qs = slice(g * GT, (g + 1) * GT)
            ks = slice(kc * 128, (kc + 1) * 128)
            sb_ = 2 * (i % 2)
            S.group("pe", [
                I("matmul", bank(sb_)[:, 0:GT], lhsT=KT[0:64, ks], rhs=QT[0:64, hp, qs], start=True, stop=True,
                  tile_position=(0, 0)),
                I("matmul", bank(sb_ + 1)[:, 0:GT], lhsT=KT[64:128, ks], rhs=QT[64:128, hp, qs], start=True, stop=True,
                  tile_position=(64, 0)),
            ], reads=[B_KT[kc], B_QT[hp][g]], writes=[PB[sb_], PB[sb_ + 1]])

        for i0 in range(min(2, n_it)):
            emit_S(i0)
        for i in range(n_it):
            hp, g, kc = its[i]
            qs = slice(g * GT, (g + 1) * GT)
            ob = 4 + 2 * ((hp * NGJ + g) % 2)
            oA, oB = bank(ob), bank(ob + 1)
            sb_ = 2 * (i % 2)
            pt = PT[i % NPT]
            bpt = B_PT[i % NPT]
            if GT == 512:
                S.op("act", I("activation", out=pt[:], in_=bank(sb_, 2), func=AF.Exp, scale=0.125),
                     reads=[PB[sb_], PB[sb_ + 1]], writes=[bpt])
            else:
                S.op("act", I("activation", out=pt[:].rearrange("p (a t) -> p a t", a=2)[:, :, 0:GT],
                              in_=bank(sb_, 2).rearrange("p (a t) -> p a t", a=2)[:, :, 0:GT], func=AF.Exp, scale=0.125),
                     reads=[PB[sb_], PB[sb_ + 1]], writes=[bpt])
            want = ((i + 1) * n_side + n_it - 1) // n_it
            while emitted < want and side:
                side.pop(0)(sb_)
                emitted += 1
            if i + 2 < n_it:
                emit_S(i + 2)
            S.group("pe", [
                I("matmul", oA[:, 0:GT], lhsT=V[:, kc, 0:128], rhs=pt[:, 0:GT], start=(kc == 0), stop=(kc == nkc - 1)),
                I("matmul", oB[:, 0:GT], lhsT=V[:, kc, 64:192], rhs=pt[:, 512:512 + GT], start=(kc == 0), stop=(kc == nkc - 1)),
            ], reads=[B_V[kc], bpt], writes=[PB[ob], PB[ob + 1]])
            if kc == nkc - 1:
                S.op("dve", I("reciprocal", out=rd[64:128, 0:GT], in_=oA[64:128, 0:GT]), reads=[PB[ob]], writes=[B_rd])
                S.op("dve", I("reciprocal", out=rd[0:64, 0:GT], in_=oB[0:64, 0:GT]), reads=[PB[ob + 1]], writes=[B_rd])
                S.op("dve", I("tensor_tensor", out=QT[0:64, hp, qs], in0=oA[0:64, 0:GT], in1=rd[64:128, 0:GT], op=ALU.mult),
                     reads=[PB[ob], B_rd], writes=[B_QT[hp][g]])
                S.op("dve", I("tensor_tensor", out=QT[64:128, hp, qs], in0=oB[64:128, 0:GT], in1=rd[0:64, 0:GT], op=ALU.mult),
                     reads=[PB[ob + 1], B_rd], writes=[B_QT[hp][g]])
        while side:
            side.pop(0)(0)

    wctr = {"n": 0}
    NWS = len(wpool)

    def stream_w(src_ap, src_buf):
        i = wctr["n"] % NWS
        wctr["n"] += 1
        S.dma("sp", I("dma_start", out=wpool[i][:], in_=src_ap), reads=[RG, src_buf], writes=[B_wpool[i]])
        return wpool[i], B_wpool[i]

    def mlp_phase(x_src, y_dst):
        for g in range(NGJ):
            def outproj(tt):
                t = g * TPG + tt
                ts_ = slice(t * 128, (t + 1) * 128)
                ob = 2 * tt
                fns = []
                for half in range(2):
                    for c in range(8):
                        lhs = QT[:, c, ts_] if c < 4 else CA[:, c - 4, ts_]
                        fns.append(I("matmul", bank(ob + half), lhsT=lhs, rhs=w_out_bf[:, c, half * 512:(half + 1) * 512],
                                     start=(c == 0), stop=(c == 7)))
                S.group("pe", fns, reads=[B_QT[c][g] for c in range(4)] + [B_CA[g], B_wout], writes=[PB[ob], PB[ob + 1]])
                sl = state["slot"]
                state["slot"] ^= 1
                S.dma("sp", I("dma_start", out=xt[sl][:], in_=x_src(t)), writes=[B_xt[sl]])
                S.op("dve", I("tensor_tensor", out=hbuf[:, tt, :], in0=bank(ob, 2), in1=xt[sl][:], op=ALU.add),
                     reads=[RG, PB[ob], PB[ob + 1], B_xt[sl]], writes=[B_h[tt]])

            def cast_T(tt):
                sl = tt % 2
                S.op("dve", I("tensor_copy", out=xnb[sl][:], in_=hbuf[:, tt, :]), reads=[RG, B_h[tt]], writes=[B_xnb[sl]])
                tb = 2 * tt
                pv = bank_bf(tb)
                S.group("pe", [I("transpose", out=pv[:, k * 128:(k + 1) * 128], in_=xnb[sl][:, k * 128:(k + 1) * 128],
                                 identity=ident[:]) for k in range(8)], reads=[B_xnb[sl], B_const], writes=[PB[tb]])
                S.op("act", I("copy", out=xnT[:, :, tt * 128:(tt + 1) * 128], in_=pv.rearrange("p (k t) -> p k t", k=8)),
                     reads=[PB[tb]], writes=[B_xnT])

            def stats(tt):
                junk = relu2.bitcast(BF16)
                S.op("act", I("activation", out=junk[:, 0:D], in_=hbuf[:, tt, :], func=AF.Square, accum_out=rs2[:, tt:tt + 1]),
                     reads=[RG, B_h[tt]], writes=[B_relu[0], B_relu[1], B_rs2[tt]])
                S.op("dve", I("tensor_scalar", out=rs2[:, tt:tt + 1], in0=rs2[:, tt:tt + 1], scalar1=1.0 / D, scalar2=EPS,
                              op0=ALU.mult, op1=ALU.add), reads=[B_rs2[tt]], writes=[B_rs2[tt]])
                S.op("dve", I("reciprocal", out=rs2[:, tt:tt + 1], in_=rs2[:, tt:tt + 1]), reads=[B_rs2[tt]], writes=[B_rs2[tt]])

            outproj(0)
            for tt in range(TPG):
                if tt + 1 < TPG:
                    outproj(tt + 1)
                cast_T(tt)
            for tt in range(TPG):
                stats(tt)
            for f in range(NFF):
                w_ap, w_buf = stream_w(wus[f], B_wus[f])
                wv = w_ap[:].rearrange("p (k c) -> p k c", k=8)
                ub = 4 + f % 4
                S.group("pe", [I("matmul", bank(ub)[:, 0:GT], lhsT=wv[:, k, :], rhs=xnT[:, k, :], start=(k == 0), stop=(k == 7))
                               for k in range(8)], reads=[RG, w_buf, B_xnT], writes=[PB[ub]])
                if f % 2 == 0:
                    S.op("act", I("activation", out=relu[0][:], in_=bank(ub)[:, 0:GT], func=AF.Relu), reads=[RG, PB[ub]], writes=[B_relu[0]])
                    S.op("act", I("activation", out=u2T[:, f, :], in_=relu[0][:], func=AF.Square), reads=[RG, B_relu[0]], writes=[B_u2T[f]])
                else:
                    S.op("dve", I("tensor_scalar", out=relu[1][:], in0=bank(ub)[:, 0:GT], scalar1=0.0, scalar2=None, op0=ALU.max),
                         reads=[RG, PB[ub]], writes=[B_relu[1]])
                    S.op("dve", I("tensor_tensor", out=u2T[:, f, :], in0=relu[1][:], in1=relu[1][:], op=ALU.mult),
                         reads=[RG, B_relu[1]], writes=[B_u2T[f]])
            for f in range(NFF):
                w_ap, w_buf = stream_w(wds[f], B_wds[f])
                fns = []
                for tt in range(TPG):
                    for half in range(2):
                        fns.append(I("matmul", bank(2 * tt + half), lhsT=u2T[:, f, tt * 128:(tt + 1) * 128],
                                     rhs=w_ap[:, half * 512:(half + 1) * 512], start=(f == 0), stop=(f == NFF - 1)))
                S.group("pe", fns, reads=[RG, B_u2T[f], w_buf], writes=[PB[i] for i in range(2 * TPG)])
            for tt in range(TPG):
                t = g * TPG + tt
                sl = tt % 2
                S.op("dve", I("scalar_tensor_tensor", out=hbuf[:, tt, :], in0=bank(2 * tt, 2), scalar=rs2[:, tt:tt + 1],
                              in1=hbuf[:, tt, :], op0=ALU.mult, op1=ALU.add),
                     reads=[RG, PB[2 * tt], PB[2 * tt + 1], B_h[tt], B_rs2[tt]], writes=[B_h[tt]])
                S.op("act", I("activation", out=xnb[sl][:], in_=hbuf[:, tt, :], func=AF.Square, accum_out=ssq[sl][:]),
                     reads=[RG, B_h[tt]], writes=[B_xnb[sl], B_st[sl]])
                S.op("act", I("activation", out=rstd[sl][:], in_=ssq[sl][:], func=AF.Sqrt, bias=EPS, scale=1.0 / D),
                     reads=[B_st[sl]], writes=[B_st[sl]])
                S.op("dve", I("reciprocal", out=rstd[sl][:], in_=rstd[sl][:]), reads=[B_st[sl]], writes=[B_st[sl]])
                S.op("dve", I("scalar_tensor_tensor", out=hbuf[:, tt, :], in0=hbuf[:, tt, :], scalar=rstd[sl][:, 0:1],
                              in1=gfin[:], op0=ALU.mult, op1=ALU.mult),
                     reads=[RG, B_h[tt], B_st[sl], B_const], writes=[B_h[tt]])
                S.dma("pool", I("dma_start", out=y_dst(t), in_=hbuf[:, tt, :]),
                      reads=[RG, B_h[tt]], writes=[B_out[octr["n"] % 8]])
                octr["n"] += 1

    def run_context(ctx_x, ctx_nt, jobs, pe_conv_ctx=False):
        if cfg.get("stop") == "pro":
            return
        first = bool(pro_thunks)
        c_items, c_st = ctx_stages(ctx_x, ctx_nt, lambda t: tab[t * 128:(t + 1) * 128, :], side=pro_thunks)
        if first or not jobs or cfg.get("stop") in ("ctx",):
            pipeline(c_items, c_st)
            c_items = []
            while pro_thunks:
                pro_thunks.pop(0)()
            region_switch()
        if cfg.get("stop") == "ctx":
            return
        for ji, (x_src, tab_src, y_dst, hl, hr) in enumerate(jobs):
            pe_conv_job = pe_conv_ctx and NGJ <= 2
            j_items, j_st = job_stages(x_src, tab_src, hl, hr, cbufb if pe_conv_job else cbuf)
            if ji == 0 and c_items:
                pipeline(c_items + j_items,
                         [(lambda i, it, a=a, b=b: (a if it[0] == "C" else b)(i, it)) for a, b in zip(c_st, j_st)])
            else:
                pipeline(j_items, j_st)
            if cfg.get("stop") == "p1":
                continue
            if pe_conv_job:
                conv_pe()
                for th_ in conv_ln_thunks(do_conv=False):
                    th_(4)
                side = []
            else:
                side = conv_ln_thunks(do_ln=False)
            if cfg.get("stop") == "conv":
                while side:
                    side.pop(0)(0)
                continue
            attn_phase(ctx_nt, side)
            if not pe_conv_job:
                for th_ in conv_ln_thunks(do_conv=False):
                    th_(0)
            if cfg.get("stop") == "attn":
                continue
            region_switch()
            mlp_phase(x_src, y_dst)
            region_switch()

    if LP > 0:
        jobs = []
        for j in range(LQ // QJ):
            base = 128 + j * QJ
            jobs.append((
                (lambda t, base=base: xq[base + t * 128: base + (t + 1) * 128, :]),
                (lambda t, j=j: tabq[j * QJ + t * 128: j * QJ + (t + 1) * 128, :]),
                (lambda t, j=j: yp[j * QJ + t * 128: j * QJ + (t + 1) * 128, :]),
                True, True))
        run_context(lambda t: xp[t * 128:(t + 1) * 128, :], LP // 128, jobs)
    for s in range(NS):
        jobs = []
        nj = LS // QJ
        for j in range(nj):
            base = s * LS + j * QJ
            jobs.append((
                (lambda t, base=base: xs[base + t * 128: base + (t + 1) * 128, :]),
                (lambda t, j=j: tab[j * QJ + t * 128: j * QJ + (t + 1) * 128, :]),
                (lambda t, base=base: ys[base + t * 128: base + (t + 1) * 128, :]),
                j > 0, j < nj - 1))
        run_context(lambda t, s=s: xs[s * LS + t * 128: s * LS + (t + 1) * 128, :], LS // 128, jobs,
                    pe_conv_ctx=cfg.get("pe_conv", True))

    S.final_wait("pool", B_out)
    S.final_wait("sp", B_out)
    S.emit()
    st.close()
    return nc, S


def rope_table(npos):
    t = np.arange(npos)
    row = (t // 64).astype(np.float32)
    col = (t % 64).astype(np.float32)
    inv_freq = (np.float32(10000.0) ** (-(np.arange(0, 32, 2, dtype=np.float32)) / np.float32(32))).astype(np.float32)
    ar = row[:, None] * inv_freq[None, :]
    ac = col[:, None] * inv_freq[None, :]
    return np.concatenate([np.cos(ar), np.cos(ac), np.sin(ar), np.sin(ac)], axis=1).astype(np.float32)


def prep_shared(cfg, norm_mix_g, w_in, q_norm_g, k_norm_g, conv_dw_w, conv_dw_b, conv_ln_g, conv_ln_b,
                w_out, norm_mlp_g, w_up, w_down, norm_final_g):
    DFF = cfg["DFF"]
    NFF = DFF // 128
    qperm = np.concatenate([np.concatenate([np.arange(c * 64, (c + 1) * 64), np.arange((4 + c) * 64, (5 + c) * 64)])
                            for c in range(4)])
    colperm = np.concatenate([qperm, np.arange(512, INW)])
    w_in_p = np.asarray(w_in[0])[:, colperm]
    rowperm = np.concatenate([qperm, np.arange(512, 1024)])
    w_out_p = np.asarray(w_out[0])[rowperm, :]
    sh = {}
    sh["w_in_h"] = np.ascontiguousarray(w_in_p.reshape(8, 128, INW).transpose(1, 0, 2))
    sh["w_out_h"] = np.ascontiguousarray(w_out_p.reshape(8, 128, D).transpose(1, 0, 2))
    sh["w_up_h"] = np.ascontiguousarray(np.asarray(w_up[0]).reshape(8, 128, NFF, 128).transpose(2, 1, 0, 3))
    sh["w_down_h"] = np.ascontiguousarray(np.asarray(w_down[0]).reshape(NFF, 128, D))
    sh["gmix"] = np.ascontiguousarray(np.asarray(norm_mix_g[0]).reshape(8, 128).T)
    sh["gmlp"] = np.ascontiguousarray(np.asarray(norm_mlp_g[0]).reshape(8, 128).T)
    gq = np.tile(np.asarray(q_norm_g[0]), 8)
    gk = np.tile(np.asarray(k_norm_g[0]), 2)
    sh["gqk"] = np.ascontiguousarray(np.broadcast_to(np.concatenate([gq, gk])[None, :], (128, 640))).astype(np.float32)
    sh["gfin"] = np.ascontiguousarray(np.broadcast_to(np.asarray(norm_final_g)[None, :], (128, D))).astype(np.float32)
    sh["convw"] = np.ascontiguousarray(np.asarray(conv_dw_w[0]).reshape(CONV_K, 4, 128).transpose(2, 1, 0))
    sh["convb"] = np.ascontiguousarray(np.asarray(conv_dw_b[0]).reshape(4, 128).T)
    sh["lng"] = np.ascontiguousarray(np.asarray(conv_ln_g[0]).reshape(4, 128).T)
    sh["lnb"] = np.ascontiguousarray(np.asarray(conv_ln_b[0]).reshape(4, 128).T)
    sh["ident"] = np.eye(128, dtype=np.float32).astype(ml_dtypes.bfloat16)
    sh["onesd"] = np.full((128, 128), 1.0 / CW, dtype=np.float32).astype(ml_dtypes.bfloat16)
    return {k: (v if v.dtype == ml_dtypes.bfloat16 else np.ascontiguousarray(v, dtype=np.float32)) for k, v in sh.items()}


_PROGRAM_CACHE = {}


def kernel(x_prompt, x_sample, norm_mix_g, w_in, q_norm_g, k_norm_g, conv_dw_w, conv_dw_b, conv_ln_g, conv_ln_b,
           w_out, norm_mlp_g, w_up, w_down, norm_final_g):
    cfg = default_cfg()
    NS, LS, LP, LQ = cfg["NS"], cfg["LS"], cfg["LP"], cfg["LQ"]
    x_prompt = np.asarray(x_prompt, dtype=np.float32)
    x_sample = np.asarray(x_sample, dtype=np.float32)
    sh = prep_shared(cfg, norm_mix_g, w_in, q_norm_g, k_norm_g, conv_dw_w, conv_dw_b, conv_ln_g, conv_ln_b,
                     w_out, norm_mlp_g, w_up, w_down, norm_final_g)
    tab_full = rope_table(max(LP, LS))
    in_maps = []
    zeros_tile = np.zeros((128, D), np.float32)
    for c in range(N_CORES):
        b, half = c // 2, c % 2
        seq = x_prompt[b]
        own = seq[half * LQ:(half + 1) * LQ]
        left = seq[half * LQ - 128: half * LQ] if half == 1 else zeros_tile
        right = seq[(half + 1) * LQ:(half + 1) * LQ + 128] if half == 0 else zeros_tile
        m = dict(sh)
        m["xs"] = np.ascontiguousarray(x_sample[c * NS:(c + 1) * NS].reshape(NS * LS, D))
        m["xp"] = np.ascontiguousarray(seq)
        m["xq"] = np.ascontiguousarray(np.concatenate([left, own, right], axis=0))
        m["tab"] = tab_full
        m["tabq"] = np.ascontiguousarray(tab_full[half * LQ:(half + 1) * LQ])
        in_maps.append(m)
    if "nc" not in _PROGRAM_CACHE:
        _PROGRAM_CACHE["nc"] = build_program(cfg)[0]
    nc = _PROGRAM_CACHE["nc"]
    res = run_bass_kernel_spmd(nc, in_maps, core_ids=list(range(N_CORES)))
    y_prompt = np.empty((4, 2 * LQ, D), np.float32)
    y_sample = np.empty((N_CORES * NS, LS, D), np.float32)
    for c in range(N_CORES):
        r = res.results[c]
        b, half = c // 2, c % 2
        y_prompt[b, half * LQ:(half + 1) * LQ] = np.asarray(r["yp"]).reshape(LQ, D)
        y_sample[c * NS:(c + 1) * NS] = np.asarray(r["ys"]).reshape(NS, LS, D)
    return (y_prompt, y_sample)
```
